# Optimizing a Trainium2 kernel written in Bass

```python
import jax, jax.numpy as jnp
from jax import lax
import numpy as np

D_MODEL = 1024
BATCH = 2
SEQ = 8192
DEPTH = 1

MLA_HEADS = 8
MLA_Q_LORA = 256
MLA_KV_LORA = 128
MLA_NOPE_DIM = 64
MLA_ROPE_DIM = 32
MLA_V_DIM = 64
FOX_HEADS = 8
FOX_HEAD_DIM = 64
D_FF = 4 * D_MODEL
Q_BLOCK = 128
ROPE_THETA = 10000.0
NORM_EPS = 1e-6
MAX_POS_OFFSET = 1024
FORGET_BIAS_LO = 1.0
FORGET_BIAS_HI = 6.0

IN_SPLITS = (MLA_Q_LORA, MLA_KV_LORA, MLA_ROPE_DIM,
             FOX_HEADS * FOX_HEAD_DIM, FOX_HEADS * FOX_HEAD_DIM, FOX_HEADS * FOX_HEAD_DIM,
             FOX_HEADS, D_MODEL, D_MODEL)
D_IN = sum(IN_SPLITS)
FORGET_COL_START = sum(IN_SPLITS[:6])

kernel_name = "hybrid_mla_fox_gated_block"


def rms_norm(x, g):
    xf = x.astype(jnp.float32)
    y = xf * lax.rsqrt(jnp.mean(xf * xf, axis=-1, keepdims=True) + NORM_EPS)
    return (y * g.astype(jnp.float32)).astype(x.dtype)


def rope(x, positions):
    half = x.shape[-1] // 2
    inv_freq = ROPE_THETA ** (-jnp.arange(half, dtype=jnp.float32) / half)
    ang = positions.astype(jnp.float32)[:, None, :, None] * inv_freq
    cos, sin = jnp.cos(ang), jnp.sin(ang)
    xf = x.astype(jnp.float32)
    x1, x2 = xf[..., :half], xf[..., half:]
    out = jnp.concatenate([x1 * cos - x2 * sin, x2 * cos + x1 * sin], axis=-1)
    return out.astype(x.dtype)


def causal_block_attention(q, k, v, scale, log_f_cum=None):
    B, H, S, _ = q.shape
    dv = v.shape[-1]
    n_blocks = S // Q_BLOCK
    kf = k.astype(jnp.float32)
    vf = v.astype(jnp.float32)
    k_pos = jnp.arange(S)

    def one_block(i):
        start = i * Q_BLOCK
        qb = lax.dynamic_slice_in_dim(q, start, Q_BLOCK, axis=2).astype(jnp.float32)
        s = jnp.einsum('bhqd,bhkd->bhqk', qb, kf) * scale
        if log_f_cum is not None:
            fq = lax.dynamic_slice_in_dim(log_f_cum, start, Q_BLOCK, axis=2)
            s = s + fq[..., :, None] - log_f_cum[..., None, :]
        q_pos = start + jnp.arange(Q_BLOCK)
        mask = k_pos[None, :] <= q_pos[:, None]
        s = jnp.where(mask, s, -jnp.inf)
        p = jax.nn.softmax(s, axis=-1)
        return jnp.einsum('bhqk,bhkd->bhqd', p, vf)

    out = lax.map(one_block, jnp.arange(n_blocks))
    out = jnp.moveaxis(out, 0, 2).reshape(B, H, S, dv)
    return out.astype(v.dtype)


def split_heads(t, n_heads):
    B, S, _ = t.shape
    return t.reshape(B, S, n_heads, -1).transpose(0, 2, 1, 3)


def merge_heads(t):
    B, H, S, d = t.shape
    return t.transpose(0, 2, 1, 3).reshape(B, S, H * d)


def setup_inputs(seed: int = 0) -> dict:
    key = jax.random.key(seed)
    ks = jax.random.split(key, 20)
    L = DEPTH

    def w(k, shape, fan_in):
        return jax.random.normal(k, shape, jnp.float32) * fan_in ** -0.5

    def gain(k, shape):
        return 1.0 + 0.05 * jax.random.normal(k, shape, jnp.float32)

    x = jax.random.normal(ks[0], (BATCH, SEQ, D_MODEL), jnp.float32)
    offset = jax.random.randint(ks[1], (BATCH, 1), 0, MAX_POS_OFFSET, dtype=jnp.int32)
    positions = (offset + jnp.arange(SEQ, dtype=jnp.int32)[None, :]).astype(jnp.int32)

    b_in = 0.02 * jax.random.normal(ks[2], (L, D_IN), jnp.float32)
    forget_bias = jnp.linspace(FORGET_BIAS_LO, FORGET_BIAS_HI, FOX_HEADS, dtype=jnp.float32)
    b_in = b_in.at[:, FORGET_COL_START:FORGET_COL_START + FOX_HEADS].add(forget_bias)

    return {
        "x": x,
        "positions": positions,
        "ln_pre_mix": gain(ks[3], (L, D_MODEL)),
        "ln_post_mix": gain(ks[4], (L, D_MODEL)),
        "ln_pre_mlp": gain(ks[5], (L, D_MODEL)),
        "ln_post_mlp": gain(ks[6], (L, D_MODEL)),
        "w_in": w(ks[7], (L, D_MODEL, D_IN), D_MODEL),
        "b_in": b_in,
        "q_a_norm": gain(ks[8], (L, MLA_Q_LORA)),
        "w_uq": w(ks[9], (L, MLA_Q_LORA, MLA_HEADS * (MLA_NOPE_DIM + MLA_ROPE_DIM)), MLA_Q_LORA),
        "kv_a_norm": gain(ks[10], (L, MLA_KV_LORA)),
        "w_uk": w(ks[11], (L, MLA_KV_LORA, MLA_HEADS * MLA_NOPE_DIM), MLA_KV_LORA),
        "w_uv": w(ks[12], (L, MLA_KV_LORA, MLA_HEADS * MLA_V_DIM), MLA_KV_LORA),
        "w_o_mla": w(ks[13], (L, MLA_HEADS * MLA_V_DIM, D_MODEL), MLA_HEADS * MLA_V_DIM),
        "w_o_fox": w(ks[14], (L, FOX_HEADS * FOX_HEAD_DIM, D_MODEL), FOX_HEADS * FOX_HEAD_DIM),
        "w_out": w(ks[15], (L, D_MODEL, D_MODEL), D_MODEL),
        "w_ff1": w(ks[16], (L, D_MODEL, D_FF), D_MODEL),
        "w_ff2": w(ks[17], (L, D_FF, D_MODEL), D_FF),
    }


def reference(x, positions, ln_pre_mix, ln_post_mix, ln_pre_mlp, ln_post_mlp,
              w_in, b_in, q_a_norm, w_uq, kv_a_norm, w_uk, w_uv,
              w_o_mla, w_o_fox, w_out, w_ff1, w_ff2):
    split_idx = [int(v) for v in np.cumsum(IN_SPLITS)[:-1]]
    for l in range(DEPTH):
        h = rms_norm(x, ln_pre_mix[l])
        z = jnp.einsum('bsd,de->bse', h, w_in[l]) + b_in[l]
        (c_q, c_kv, k_r, fq, fk, fv, f_logit, ga_logit, gb_logit) = jnp.split(z, split_idx, axis=-1)

        c_q = rms_norm(c_q, q_a_norm[l])
        q_full = split_heads(jnp.einsum('bsr,re->bse', c_q, w_uq[l]), MLA_HEADS)
        q_nope, q_rot = q_full[..., :MLA_NOPE_DIM], q_full[..., MLA_NOPE_DIM:]
        q_rot = rope(q_rot, positions)
        c_kv = rms_norm(c_kv, kv_a_norm[l])
        k_nope = split_heads(jnp.einsum('bsr,re->bse', c_kv, w_uk[l]), MLA_HEADS)
        v_mla = split_heads(jnp.einsum('bsr,re->bse', c_kv, w_uv[l]), MLA_HEADS)
        k_rot = rope(k_r[:, None, :, :], positions)
        B, H, S, _ = k_nope.shape
        q_mla = jnp.concatenate([q_nope, q_rot], axis=-1)
        k_mla = jnp.concatenate([k_nope, jnp.broadcast_to(k_rot, (B, H, S, MLA_ROPE_DIM))], axis=-1)
        o_mla = causal_block_attention(q_mla, k_mla, v_mla, (MLA_NOPE_DIM + MLA_ROPE_DIM) ** -0.5)
        y_mla = jnp.einsum('bse,ed->bsd', merge_heads(o_mla), w_o_mla[l])

        q_fox = split_heads(fq, FOX_HEADS)
        k_fox = split_heads(fk, FOX_HEADS)
        v_fox = split_heads(fv, FOX_HEADS)
        log_f = jax.nn.log_sigmoid(f_logit.astype(jnp.float32))
        log_f_cum = jnp.cumsum(log_f, axis=1).transpose(0, 2, 1)
        o_fox = causal_block_attention(q_fox, k_fox, v_fox, FOX_HEAD_DIM ** -0.5, log_f_cum)
        y_fox = jnp.einsum('bse,ed->bsd', merge_heads(o_fox), w_o_fox[l])

        merged = jax.nn.sigmoid(ga_logit) * y_mla + jax.nn.sigmoid(gb_logit) * y_fox
        mix_out = jnp.einsum('bsd,de->bse', merged, w_out[l])
        x = x + rms_norm(mix_out, ln_post_mix[l])

        h2 = rms_norm(x, ln_pre_mlp[l])
        u = jnp.einsum('bsd,df->bsf', h2, w_ff1[l])
        m = jnp.einsum('bsf,fd->bsd', jnp.square(jax.nn.relu(u)), w_ff2[l])
        x = x + rms_norm(m, ln_post_mlp[l])
    return x
```

```python
import contextlib
import types
import numpy as np
import ml_dtypes
import concourse.bass as bass
import concourse.mybir as mybir
from concourse.bass_utils import run_bass_kernel_spmd

F32 = mybir.dt.float32
BF16 = mybir.dt.bfloat16
I32 = mybir.dt.int32
AF = mybir.ActivationFunctionType
ALU = mybir.AluOpType

S = 8192
D = 1024
NOWN = 2048
DFF = 4096
MASKV = -3.0e5
RESCHED_SEGS = (0, 1, 3, 4)
EPS = 1e-6
PI = float(np.pi)

C_CQ, C_CKV, C_KR, C_FQ, C_FK, C_FV, C_F, C_GA, C_GB = 0, 256, 384, 416, 928, 1440, 1952, 1960, 2984


class Tok:
    __slots__ = ("writers", "readers", "gen_deps")

    def __init__(self):
        self.writers = []
        self.readers = []
        self.gen_deps = []


def _snap(f):
    if f.__closure__ is None:
        return f
    cells = []
    for c in f.__closure__:
        try:
            cells.append(types.CellType(c.cell_contents))
        except ValueError:
            cells.append(c)
    return types.FunctionType(f.__code__, f.__globals__, f.__name__, f.__defaults__, tuple(cells))


_DEF_N = {"pe": 512, "act": 512, "dve": 512, "pool": 1024, "sp": 512}


def _cost(eng, dma, n):
    if dma:
        return 600.0 if eng == "pool" else 60.0
    if n is None:
        n = _DEF_N[eng]
    if eng == "pe":
        return max(n, 64) / 2.4 + 4.0
    if eng == "act":
        return n / 1.2 + 170.0
    if eng == "dve":
        return n / 0.96 + 100.0
    if eng == "pool":
        return n / 0.9 + 150.0
    return 50.0


class Ins:
    __slots__ = ("eng", "fn", "idx", "deps", "sig", "signo", "dma", "dsem", "dval", "dprev", "cost", "seg", "bar", "nosig")

    def __init__(self, eng, fn, dma):
        self.eng = eng
        self.fn = fn
        self.dma = dma
        self.cost = 0.0
        self.seg = 0
        self.bar = False
        self.nosig = False
        self.deps = []
        self.sig = False
        self.signo = 0
        self.dsem = None
        self.dval = 0
        self.dprev = None


class Prog:
    ENGS = ("pe", "act", "dve", "pool", "sp")
    NRING = 12

    def __init__(self):
        self.ins = []
        self.last = {e: None for e in self.ENGS}
        self.dmas_since_barrier = []
        self.seg = 0

    def _add(self, eng, fn, reads, writes, dma, cow=(), track=True, n=None):
        i = Ins(eng, _snap(fn), dma)
        i.idx = len(self.ins)
        i.cost = _cost(eng, dma, n)
        i.seg = self.seg
        i.nosig = not track
        deps = {}
        for t in reads:
            for w in t.writers:
                deps[w.idx] = w
        for t in writes:
            g = list(t.writers) + list(t.readers)
            for w in g:
                deps[w.idx] = w
            t.gen_deps = g
        for t in cow:
            for w in t.gen_deps:
                deps[w.idx] = w
        for t in reads:
            t.readers.append(i)
        for t in writes:
            t.writers = [i]
            t.readers = []
        for t in cow:
            t.writers.append(i)
        deps.pop(i.idx, None)
        i.deps = list(deps.values())
        self.ins.append(i)
        if dma:
            self.dmas_since_barrier.append(i)
        elif track:
            self.last[eng] = i
        return i

    def op(self, eng, fn, reads=(), writes=(), cow=(), track=True, n=None):
        return self._add(eng, fn, reads, writes, False, cow, track, n)

    def dma(self, q, fn, reads=(), writes=(), cow=()):
        return self._add(q, fn, reads, writes, True, cow)

    def reschedule(self, segs):
        by_seg = {}
        for i in self.ins:
            by_seg.setdefault(i.seg, []).append(i)
        new = []
        for sg in sorted(by_seg):
            lst = by_seg[sg]
            if sg in segs:
                lst = self._list_schedule(lst)
            new.extend(lst)
        self.ins = new
        for k, i in enumerate(new):
            i.idx = k

    def _list_schedule(self, lst):
        LAT = 150.0
        DMA_LAT = 2500.0
        bars = [i for i in lst if i.bar]
        ops = [i for i in lst if not i.bar]
        inseg = {id(i) for i in ops}
        users = {id(i): [] for i in ops}
        indeg = {}
        for i in ops:
            k = 0
            for d in i.deps:
                if id(d) in inseg:
                    users[id(d)].append(i)
                    k += 1
            indeg[id(i)] = k
        start, fin = {}, {}
        depready = {}
        ready = {e: [] for e in self.ENGS}
        for i in ops:
            if indeg[id(i)] == 0:
                depready[id(i)] = 0.0
                ready[i.eng].append(i)
        eng_free = {e: 0.0 for e in self.ENGS}
        dma_hist = {e: [] for e in self.ENGS}
        order = []
        remaining = len(ops)
        while remaining:
            best = None
            for e in self.ENGS:
                rl = ready[e]
                if not rl:
                    continue
                ef = eng_free[e]
                cand = None
                for i in rl:
                    st = depready[id(i)]
                    if i.dma:
                        h = dma_hist[e]
                        if len(h) >= self.NRING:
                            st = max(st, h[len(h) - self.NRING])
                    st = max(st, ef)
                    key = (st, i.idx)
                    if cand is None or key < cand[0]:
                        cand = (key, i)
                if best is None or cand[0] < best[0]:
                    best = cand
            (st, _), i = best
            ready[i.eng].remove(i)
            start[id(i)] = st
            if i.dma:
                eng_free[i.eng] = st + i.cost
                f = st + i.cost + DMA_LAT
                dma_hist[i.eng].append(f)
            else:
                f = st + i.cost
                eng_free[i.eng] = f
            fin[id(i)] = f
            order.append(i)
            remaining -= 1
            for u in users[id(i)]:
                indeg[id(u)] -= 1
                if indeg[id(u)] == 0:
                    dr = 0.0
                    for d in u.deps:
                        if id(d) in inseg:
                            if d.eng == u.eng == "pe" and not u.dma:
                                t = fin[id(d)]
                            else:
                                t = fin[id(d)] + LAT
                            if t > dr:
                                dr = t
                    depready[id(u)] = dr
                    ready[u.eng].append(u)
        self.sim_end = max(fin.values()) if fin else 0.0
        lasts = {}
        for i in order:
            if not i.dma and not i.nosig:
                lasts[i.eng] = i
        dmas = [i for i in order if i.dma]
        for b_ in bars:
            b_.deps = [v for e, v in lasts.items() if e != b_.eng] + dmas
        return order + bars

    def mark(self, tok):
        tok.writers.append(self.ins[-1])

    def barrier(self):
        lasts = [v for v in self.last.values() if v is not None]
        dmas = list(self.dmas_since_barrier)
        self.dmas_since_barrier = []
        for e in self.ENGS:
            i = Ins(e, lambda eng: None, False)
            i.idx = len(self.ins)
            i.deps = [d for d in lasts if d.eng != e] + dmas
            i.seg = self.seg
            i.bar = True
            self.ins.append(i)
        self.seg += 1

    def emit(self, sems):
        per_eng = {e: [] for e in self.ENGS}
        for i in self.ins:
            per_eng[i.eng].append(i)
        for i in self.ins:
            for d in i.deps:
                if d.dma:
                    continue
                if d.eng == i.eng and not i.dma and d.eng == "pe":
                    continue
                d.sig = True
        cnt = {e: 0 for e in self.ENGS}
        dcount = {e: 0 for e in self.ENGS}
        lastring = {e: {} for e in self.ENGS}
        for i in self.ins:
            if i.dma:
                k = dcount[i.eng]
                dcount[i.eng] += 1
                slot = k % self.NRING
                i.dsem = sems["ring"][i.eng][slot]
                i.dval = 16 * (k // self.NRING + 1)
                i.dprev = lastring[i.eng].get(slot)
                lastring[i.eng][slot] = i
            elif i.sig:
                cnt[i.eng] += 1
                i.signo = cnt[i.eng]

        def run_engine(engname, eng):
            waited = {}
            for i in per_eng[engname]:
                need = {}
                deps = list(i.deps)
                if i.dma and i.dprev is not None:
                    deps.append(i.dprev)
                for d in deps:
                    if d.dma:
                        s, v = d.dsem, d.dval
                    else:
                        if d.eng == engname and not i.dma and engname == "pe":
                            continue
                        s, v = sems["cnt"][d.eng], d.signo
                    key = id(s)
                    if need.get(key, (None, 0))[1] < v:
                        need[key] = (s, v)
                for key, (s, v) in need.items():
                    if waited.get(key, 0) >= v:
                        continue
                    eng.wait_ge(s, v)
                    waited[key] = v
                r = i.fn(eng)
                if r is None:
                    continue
                if i.dma:
                    r.then_inc(i.dsem, 16)
                elif i.sig:
                    r.then_inc(sems["cnt"][engname], 1)

        return run_engine


class Ring:
    def __init__(self, n):
        self.n = n
        self.toks = [Tok() for _ in range(n)]
        self.i = -1

    def next(self):
        self.i += 1
        k = self.i % self.n
        return k, self.toks[k]


def build_program(dbg=False):
    nc = bass.Bass("TRN2", target_bir_lowering=False)
    P = Prog()

    def din(name, shape, dt=F32):
        return nc.dram_tensor(name, list(shape), dt, kind="ExternalInput")

    def dscr(name, shape, dt):
        return nc.dram_tensor(name, list(shape), dt, kind=("ExternalOutput" if dbg else "Internal"))

    xb_h = din("xb", [S, D])
    xo_h = din("xo", [NOWN, D])
    posb_h = din("posb", [128, 64], I32)
    poso_h = din("poso", [128, 16], I32)
    sel_h = din("sel", [8, 4])
    mask_h = din("maskT", [128, 16 * 128], BF16)
    ropec_h = din("ropec", [128, 64])
    w_in_h = din("w_in", [D, 4008])
    b_in_h = din("b_in", [4008])
    g_pre_mix_h = din("ln_pre_mix", [D])
    g_post_mix_h = din("ln_post_mix", [D])
    g_pre_mlp_h = din("ln_pre_mlp", [D])
    g_post_mlp_h = din("ln_post_mlp", [D])
    q_a_norm_h = din("q_a_norm", [256])
    kv_a_norm_h = din("kv_a_norm", [128])
    w_uq_h = din("w_uq", [256, 768])
    w_uk_h = din("w_uk", [128, 512])
    w_uv_h = din("w_uv", [128, 512])
    w_om_h = din("w_o_mla", [512, D])
    w_of_h = din("w_o_fox", [512, D])
    w_out_h = din("w_out", [D, D])
    w_ff1_h = din("w_ff1", [D, DFF])
    w_ff2_h = din("w_ff2", [DFF, D])
    out_h = nc.dram_tensor("out", [NOWN, D], F32, kind="ExternalOutput")

    KTf = dscr("KTf", [8, 70, S], BF16).ap()
    Vf = dscr("Vf", [8, 128, 64 * 65], BF16).ap()
    KTm = dscr("KTm", [8, 96, S], BF16).ap()
    Vm = dscr("Vm", [8, 128, 64 * 65], BF16).ap()
    QTf = dscr("QTf", [8, 70, NOWN], BF16).ap()
    QTm = dscr("QTm", [8, 96, NOWN], BF16).ap()
    SG = dscr("SG", [2, 8, 128, NOWN], BF16).ap()
    OT = dscr("OT", [2, 512, NOWN], BF16).ap()
    X1 = dscr("X1", [NOWN, D], F32).ap()
    H2T = dscr("H2T", [4, 128, 8 * 512], BF16).ap()
    WGb = nc.dram_tensor("WGb", [D, 2048], BF16).ap()
    FF2b = nc.dram_tensor("FF2b", [DFF, D], BF16).ap()
    t_WGb, t_FF2b = Tok(), Tok()
    t_KTf, t_Vf, t_KTm, t_Vm, t_QTf, t_QTm, t_SG, t_OT, t_X1, t_H2T, t_out = [Tok() for _ in range(11)]

    xb, xo = xb_h.ap(), xo_h.ap()
    w_in, b_in = w_in_h.ap(), b_in_h.ap()
    outp = out_h.ap()

    def bcast_rows(h, off, n):
        return bass.AP(h, off, [[0, 128], [1, n]])

    def colvec(h, off, nchunks, p=128):
        return bass.AP(h, off, [[1, p], [p, nchunks]])

    def mk_alloc(es):
        def sb(name, shape, dt):
            return es.enter_context(nc.sbuf_tensor(name, list(shape), dt))
        return sb

    with contextlib.ExitStack() as es0:
        sb0 = mk_alloc(es0)
        sems = {"cnt": {}, "ring": {}}
        for e in Prog.ENGS:
            sems["cnt"][e] = es0.enter_context(nc.semaphore("c_" + e))
            sems["ring"][e] = [es0.enter_context(nc.semaphore(f"r_{e}{k}")) for k in range(Prog.NRING)]

        ident = sb0("ident", [128, 128], BF16)
        identf = sb0("identf", [128, 128], F32)
        onesf = sb0("onesf", [128, 64], F32)
        cst = sb0("cst", [128, 4], F32)
        stat = sb0("stat", [128, 64], F32)
        junk = sb0("junk", [128, D], BF16)
        t_ident, t_identf, t_onesf, t_cst = [Tok() for _ in range(4)]
        gain_hs = [g_pre_mix_h, g_post_mix_h, g_pre_mlp_h, g_post_mlp_h]

        def load_gains(sbx, name, idxs):
            gt = sbx(name, [128, len(idxs), D], F32)
            tk = Tok()
            for k, gi in enumerate(idxs):
                P.dma("sp", lambda e: e.dma_start(out=gt[:, k, :], in_=bcast_rows(gain_hs[gi], 0, D)), [], [], [tk])
            return gt, tk
        stat_ring = Ring(16)

        P.op("pool", lambda e: e.memset(identf[:], 0.0), [], [t_identf])
        P.op("pool", lambda e: e.affine_select(out=identf[:], in_=identf[:], pattern=[[-1, 128]], compare_op=ALU.not_equal,
                                              fill=1.0, base=0, channel_multiplier=1), [t_identf], [t_identf])
        P.op("dve", lambda e: e.tensor_copy(out=ident[:], in_=identf[:]), [t_identf], [t_ident])
        P.op("dve", lambda e: e.memset(onesf[:], 1.0), [], [t_onesf])
        P.op("dve", lambda e: e.memset(cst[:, 0:1], EPS), [], [t_cst])
        P.op("dve", lambda e: e.memset(cst[:, 1:2], 1.0), [t_cst], [t_cst])
        P.op("dve", lambda e: e.memset(cst[:, 2:3], 0.0), [t_cst], [t_cst])
        P.op("dve", lambda e: e.memset(cst[:, 3:4], -1.0), [t_cst], [t_cst])

        def rstd_from_ss(ss_ap, n_feat, reads, rs_out, tok_out):
            P.op("act", lambda e: e.activation(out=rs_out[:, 1:2], in_=ss_ap, func=AF.Ln, scale=1.0 / n_feat, bias=cst[:, 0:1]),
                 reads + [t_cst], [tok_out], n=1)
            P.op("act", lambda e: e.activation(out=rs_out[:, 0:1], in_=rs_out[:, 1:2], func=AF.Exp, scale=-0.5), [tok_out], [tok_out], n=1)

        evac_flip = [0]

        def evac_copy(out_ap, in_ap, reads, writes, eng=None, cow=(), n=512):
            if eng is None:
                evac_flip[0] ^= 1
                eng = "act" if evac_flip[0] else "dve"
            if eng == "act":
                P.op("act", lambda e: e.activation(out=out_ap, in_=in_ap, func=AF.Copy), reads, writes, cow, n=n)
            else:
                P.op("dve", lambda e: e.tensor_copy(out=out_ap, in_=in_ap), reads, writes, cow, n=n)

        def new_stat():
            k, st_tok = stat_ring.next()
            return stat[:, 4 * k:4 * k + 4], st_tok

        with contextlib.ExitStack() as esP12:
            pa = [esP12.enter_context(nc.psum_tensor(f"pa{k}", [128, 512], F32)) for k in range(6)]
            ptb = [esP12.enter_context(nc.psum_tensor(f"pt{k}", [128, 1024], BF16)) for k in range(2)]
            pa_ring = Ring(6)
            pt_ring = Ring(2)
            esA = esP12
            sbA = mk_alloc(esA)
            C_all = sbA("C_all", [8, S], F32)
            t_Call = [Tok() for _ in range(16)]
            XS = 4
            xt = sbA("xt", [128, XS, D], F32)
            xt_ring = Ring(XS)
            xn = sbA("xn", [128, 3, D], BF16)
            xn_ring = Ring(3)
            hT = sbA("hT", [128, 2, 8, 512], BF16)
            hT_toks = [[Tok() for _ in range(4)] for _ in range(2)]
            gbcA, t_gbcA = load_gains(sbA, "gbcA", [0])

            def normA(x_ap, x_tok, g_ap, g_tok, xn_t, ring):
                st, st_tok = new_stat()
                P.op("act", lambda e: e.activation(out=junk[:], in_=x_ap, func=AF.Square, accum_out=st[:, 0:1]), [x_tok], [st_tok], n=1024)
                rstd_from_ss(st[:, 0:1], D, [st_tok], st[:, 2:4], st_tok)
                xk, xtok = ring.next()
                P.op("dve", lambda e: e.scalar_tensor_tensor(out=xn_t[:, xk, :], in0=x_ap, scalar=st[:, 2:3], in1=g_ap,
                                                            op0=ALU.mult, op1=ALU.mult), [x_tok, st_tok, g_tok], [xtok], n=1024)
                return xk, xtok

            def normB(xn_t, xk, xtok, dst_ap, dst_tok, pt_list, ptring):
                pk, ptok = ptring.next()
                for fc in range(8):
                    P.op("pe", lambda e: e.transpose(out=pt_list[pk][:, fc * 128:(fc + 1) * 128], in_=xn_t[:, xk, fc * 128:(fc + 1) * 128],
                                                    identity=ident[:]), [xtok, t_ident], [ptok], n=128)
                evac_copy(dst_ap, pt_list[pk][:].rearrange("p (c t) -> p c t", c=8), [ptok], [dst_tok], n=1024)

            rowt = sbA("rowt", [16, 128], F32)
            t_rowt = Tok()

            def load_colvec(dst_ap, h, off, C, reads_tok_cow):
                P.dma("sp", lambda e: e.dma_start(out=rowt[0:C, :], in_=bass.AP(h, off, [[128, C], [1, 128]])), [], [t_rowt])
                pk, ptok = pa_ring.next()
                P.op("pe", lambda e: e.matmul(pa[pk][:, 0:C], lhsT=rowt[0:C, :], rhs=identf[0:C, 0:C], start=True, stop=True), [t_rowt, t_identf], [ptok], n=64)
                P.op("dve", lambda e: e.tensor_copy(out=dst_ap, in_=pa[pk][:, 0:C]), [ptok], [], [reads_tok_cow], n=16)

            tilesA, normed = {}, {}

            def sA_load(ci):
                src, r0 = (xb, ci * 512) if ci < 16 else (xo, (ci - 16) * 512)
                lst = []
                for t in range(4):
                    k, tk = xt_ring.next()
                    P.dma("sp", lambda e: e.dma_start(out=xt[:, k, :], in_=src[r0 + t * 128:r0 + (t + 1) * 128, :]), [], [tk])
                    lst.append((k, tk))
                tilesA[ci] = lst

            def sA_normA(ci, t):
                k, tk = tilesA[ci][t]
                normed[(ci, t)] = normA(xt[:, k, :], tk, gbcA[:, 0, :], t_gbcA, xn, xn_ring)

            def sA_normB(ci, t):
                xk, xtok = normed[(ci, t)]
                normB(xn, xk, xtok, hT[:, ci % 2, :, t * 128:(t + 1) * 128], hT_toks[ci % 2][t], ptb, pt_ring)

            def rope_table(sbx, name, pos_h, NT):
                posi = sbx(name + "_pi", [128, NT], I32)
                posf = sbx(name + "_pf", [128, NT], F32)
                rc = sbx(name + "_rc", [128, 64], F32)
                ang = sbx(name + "_ang", [128, NT, 32], F32)
                tq = sbx(name + "_tq", [128, NT, 32], F32)
                ki = sbx(name + "_ki", [128, NT, 32], I32)
                tk = Tok()
                tcs = Tok()
                P.dma("sp", lambda e: e.dma_start(out=posi[:], in_=pos_h.ap()), [], [tk])
                P.dma("sp", lambda e: e.dma_start(out=rc[:], in_=ropec_h.ap()), [], [], [tk])
                P.op("dve", lambda e: e.tensor_copy(out=posf[:], in_=posi[:]), [tk], [tk])
                posb3 = bass.AP(posf, 0, [[NT, 128], [1, NT], [0, 32]])
                invb3 = bass.AP(rc, 0, [[64, 128], [0, NT], [1, 32]])
                shb3 = bass.AP(rc, 32, [[64, 128], [0, NT], [1, 32]])
                P.op("dve", lambda e: e.tensor_tensor(out=ang[:], in0=posb3, in1=invb3, op=ALU.mult), [tk], [tk])
                P.op("dve", lambda e: e.tensor_tensor(out=ang[:], in0=ang[:], in1=shb3, op=ALU.add), [tk], [tk])
                P.op("dve", lambda e: e.tensor_scalar(out=tq[:], in0=ang[:], scalar1=float(1.0 / (2 * PI)), scalar2=None, op0=ALU.mult), [tk], [tk])
                P.op("dve", lambda e: e.tensor_copy(out=ki[:], in_=tq[:]), [tk], [tk])
                P.op("dve", lambda e: e.tensor_copy(out=tq[:], in_=ki[:]), [tk], [tk])
                P.op("dve", lambda e: e.scalar_tensor_tensor(out=ang[:], in0=tq[:], scalar=float(-2 * PI), in1=ang[:], op0=ALU.mult, op1=ALU.add), [tk], [tk])
                P.op("dve", lambda e: e.tensor_scalar(out=ang[:], in0=ang[:], scalar1=-PI, scalar2=PI, op0=ALU.max, op1=ALU.min), [tk], [tk])
                P.op("act", lambda e: e.activation(out=tq[:], in_=ang[:], func=AF.Sin), [tk], [tcs])
                return tq, tcs

            wQ_s = sbA("wQ", [128, 8, 768], BF16)
            wuq_s = sbA("wuq", [128, 2, 768], BF16)
            t_wQ_s, t_wuq_s = Tok(), Tok()
            sA_load(0)
            esA1 = contextlib.ExitStack()
            sb1 = mk_alloc(esA1)
            wK = sb1("wK", [128, 8, 1192], BF16)
            wuk = sb1("wuk", [128, 512], BF16)
            wuv = sb1("wuv", [128, 512], BF16)
            t_wK, t_wuk, t_wuv = Tok(), Tok(), Tok()
            for fc in range(8):
                P.dma("pool", lambda e: e.dma_start(out=wK[:, fc, 0:1032], in_=w_in[fc * 128:(fc + 1) * 128, C_FK:C_FK + 1032]), [], [], [t_wK])
                P.dma("pool", lambda e: e.dma_start(out=wK[:, fc, 1032:1192], in_=w_in[fc * 128:(fc + 1) * 128, C_CKV:C_CKV + 160]), [], [], [t_wK])
            P.dma("pool", lambda e: e.dma_start(out=wuk[:], in_=w_uk_h.ap()), [], [], [t_wuk])
            P.dma("pool", lambda e: e.dma_start(out=wuv[:], in_=w_uv_h.ap()), [], [], [t_wuv])
            t_mark1 = Tok()
            p2_pref = []
            for fc in range(8):
                p2_pref.append((lambda e, fc=fc: e.dma_start(out=wQ_s[:, fc, 0:512], in_=w_in[fc * 128:(fc + 1) * 128, C_FQ:C_FQ + 512]), t_wQ_s))
                p2_pref.append((lambda e, fc=fc: e.dma_start(out=wQ_s[:, fc, 512:768], in_=w_in[fc * 128:(fc + 1) * 128, C_CQ:C_CQ + 256]), t_wQ_s))
            for lc in range(2):
                p2_pref.append((lambda e, lc=lc: e.dma_start(out=wuq_s[:, lc, :], in_=w_uq_h.ap()[lc * 128:(lc + 1) * 128, :]), t_wuq_s))
            for fc in range(8):
                p2_pref.append((lambda e, fc=fc: e.dma_start(out=WGb[fc * 128:(fc + 1) * 128, :], in_=w_in[fc * 128:(fc + 1) * 128, C_GA:C_GA + 2048]), t_WGb))

            def emit_p2_prefetch(k):
                mk = Tok()
                P.mark(mk)
                for _ in range(k):
                    if p2_pref:
                        fn_, tk_ = p2_pref.pop(0)
                        P.dma("pool", fn_, [mk], [], [tk_])

            b_fk = sb1("b_fk", [128, 4], F32)
            nb_f = sb1("nb_f", [8, 2], F32)
            b_fv = sb1("b_fv", [128, 512], F32)
            b_ckr = sb1("b_ckr", [128, 160], F32)
            g_kv = sb1("g_kv", [128, 1], F32)
            t_bias1 = Tok()
            load_colvec(b_fk[:], b_in_h, C_FK, 4, t_bias1)
            P.dma("sp", lambda e: e.dma_start(out=nb_f[:, 0:1], in_=colvec(b_in_h, C_F, 1, p=8), allow_slow_non_contiguous=True), [], [], [t_bias1])
            P.dma("sp", lambda e: e.dma_start(out=b_fv[:], in_=bcast_rows(b_in_h, C_FV, 512)), [], [], [t_bias1])
            P.dma("sp", lambda e: e.dma_start(out=b_ckr[:], in_=bcast_rows(b_in_h, C_CKV, 160)), [], [], [t_bias1])
            P.dma("sp", lambda e: e.dma_start(out=g_kv[:], in_=colvec(kv_a_norm_h, 0, 1), allow_slow_non_contiguous=True), [], [], [t_bias1])
            t_nbf = Tok()
            P.op("dve", lambda e: e.tensor_scalar(out=nb_f[:, 1:2], in0=nb_f[:, 0:1], scalar1=-1.0, scalar2=None, op0=ALU.mult), [t_bias1], [t_nbf])
            rope_hold = {}

            kst = sb1("kst", [128, 4, 512], BF16)
            kst_ring = Ring(4)
            vst = sb1("vst", [128, 2, 8, 260], BF16)
            vst_toks = [Tok(), Tok()]
            vmst = sb1("vmst", [128, 2, 8, 260], BF16)
            vmst_toks = [Tok(), Tok()]
            for s_ in range(2):
                P.op("pool", lambda e: e.memset(vst[:, s_], 1.0), [], [vst_toks[s_]])
                P.op("pool", lambda e: e.memset(vmst[:, s_], 1.0), [], [vmst_toks[s_]])
            ckr = sb1("ckr", [128, 2, 4, 160], F32)
            ckr_toks = [[Tok() for _ in range(4)] for _ in range(2)]
            ckn = sb1("ckn", [128, 2, 4, 128], BF16)
            ckn_toks = [[Tok() for _ in range(4)] for _ in range(2)]
            ckvT = sb1("ckvT", [128, 2, 512], BF16)
            ckvT_toks = [[Tok() for _ in range(4)] for _ in range(2)]
            rtmp = sb1("rtmp", [128, 4, 4, 16], F32)
            t_rtmp = Tok()
            krot = sb1("krot", [128, 2, 4, 32], BF16)
            krot_toks = [Tok(), Tok()]
            krT = sb1("krT", [32, 2, 512], BF16)
            krT_toks = [Tok(), Tok()]
            knst = sb1("knst", [128, 4, 512], BF16)
            knst_ring = Ring(4)
            fg = sb1("fg", [8, 2, 512], F32)
            t_fg = Tok()
            fones = sb1("fones", [8, 512], F32)
            t_fones = Tok()
            P.op("pool", lambda e: e.memset(fones[:], 1.0), [], [t_fones])
            fw_ = sb1("fw", [8, 2, 2, 512], F32)
            t_fw = [Tok(), Tok()]
            f6 = sb1("f6", [8, 2, 6, 512], BF16)
            f6_toks = [Tok(), Tok()]
            for s_ in range(2):
                P.op("pool", lambda e: e.memset(f6[:, s_], 1.0), [], [f6_toks[s_]])

            def split3(src_ap, dst3, reads, dst_tok, mul, wt, wtok):
                r1, r2 = wt[:, 0, :], wt[:, 1, :]

                def st_a():
                    P.op("dve", lambda e: e.tensor_scalar(out=dst3[:, 0, :], in0=src_ap, scalar1=mul, scalar2=None, op0=ALU.mult), reads, [dst_tok])
                    P.op("dve", lambda e: e.scalar_tensor_tensor(out=r1, in0=src_ap, scalar=mul, in1=dst3[:, 0, :], op0=ALU.mult, op1=ALU.subtract),
                         reads + [dst_tok], [wtok])

                def st_b():
                    P.op("dve", lambda e: e.tensor_copy(out=dst3[:, 1, :], in_=r1), [wtok], [], [dst_tok])
                    P.op("dve", lambda e: e.tensor_tensor(out=r2, in0=r1, in1=dst3[:, 1, :], op=ALU.subtract), [wtok, dst_tok], [], [wtok])

                def st_c():
                    P.op("dve", lambda e: e.tensor_copy(out=dst3[:, 2, :], in_=r2), [wtok], [], [dst_tok])
                return [st_a, st_b, st_c]

            pend_split = []

            def drain_split(n=1):
                for _ in range(n):
                    if pend_split:
                        pend_split.pop(0)()

            def p1_slots(c):
                hs = c % 2
                cs_ = c % 2
                vs = c % 2
                cols = slice(c * 512, (c + 1) * 512)
                hTr = hT_toks[hs]

                def kproj(m):
                    pk, ptok = pa_ring.next()
                    for fc in range(8):
                        P.op("pe", lambda e: e.matmul(pa[pk][:], lhsT=wK[:, fc, m * 128:(m + 1) * 128], rhs=hT[:, hs, fc, :],
                                                      start=(fc == 0), stop=(fc == 7)), hTr + [t_wK], [ptok])
                    sk, stok = kst_ring.next()
                    P.op("act", lambda e: e.activation(out=kst[:, sk, :], in_=pa[pk][:], func=AF.Identity, bias=b_fk[:, m:m + 1]),
                         [ptok, t_bias1], [stok])
                    for hh in range(2):
                        P.dma("sp", lambda e: e.dma_start(out=KTf[2 * m + hh, 0:64, cols], in_=kst[hh * 64:(hh + 1) * 64, sk, :]), [stok], [], [t_KTf])

                def vproj(t):
                    pk, ptok = pa_ring.next()
                    for fc in range(8):
                        P.op("pe", lambda e: e.matmul(pa[pk][:], lhsT=hT[:, hs, fc, t * 128:(t + 1) * 128], rhs=wK[:, fc, 512:1024],
                                                      start=(fc == 0), stop=(fc == 7)), [hTr[t], t_wK], [ptok])
                    dst = bass.AP(vst, vs * 8 * 260 + t * 65, [[2 * 8 * 260, 128], [260, 8], [1, 64]])
                    P.op("dve", lambda e: e.tensor_tensor(out=dst, in0=pa[pk][:].rearrange("p (h d) -> p h d", h=8),
                                                         in1=b_fv[:].rearrange("p (h d) -> p h d", h=8), op=ALU.add),
                         [ptok, t_bias1], ([vst_toks[vs]] if t == 0 else []), ([] if t == 0 else [vst_toks[vs]]))
                    if t == 3:
                        P.dma("sp", lambda e: e.dma_start(out=Vf[:, :, c * 260:(c + 1) * 260].rearrange("h p x -> p h x"), in_=vst[:, vs]), [vst_toks[vs]], [], [t_Vf])

                def ckr_mm(t):
                    pk, ptok = pa_ring.next()
                    for fc in range(8):
                        P.op("pe", lambda e: e.matmul(pa[pk][:, 0:160], lhsT=hT[:, hs, fc, t * 128:(t + 1) * 128], rhs=wK[:, fc, 1032:1192],
                                                      start=(fc == 0), stop=(fc == 7)), [hTr[t], t_wK], [ptok], n=160)
                    P.op("dve", lambda e: e.tensor_tensor(out=ckr[:, cs_, t, :], in0=pa[pk][:, 0:160], in1=b_ckr[:], op=ALU.add),
                         [ptok, t_bias1], [ckr_toks[cs_][t]], n=160)
                    st, st_tok = new_stat()
                    P.op("act", lambda e: e.activation(out=junk[:, 0:128], in_=ckr[:, cs_, t, 0:128], func=AF.Square, accum_out=st[:, 0:1]),
                         [ckr_toks[cs_][t]], [st_tok], n=128)
                    rstd_from_ss(st[:, 0:1], 128, [st_tok], st[:, 2:4], st_tok)
                    P.op("dve", lambda e: e.tensor_scalar(out=ckn[:, cs_, t, :], in0=ckr[:, cs_, t, 0:128], scalar1=st[:, 2:3], scalar2=None, op0=ALU.mult),
                         [ckr_toks[cs_][t], st_tok], [ckn_toks[cs_][t]], n=128)

                def ckn_T():
                    tpk, tptok = pt_ring.next()
                    for t in range(4):
                        P.op("pe", lambda e: e.transpose(out=ptb[tpk][:, t * 128:(t + 1) * 128], in_=ckn[:, cs_, t, :], identity=ident[:]),
                             [ckn_toks[cs_][t], t_ident], [tptok], n=128)
                    P.op("dve", lambda e: e.tensor_scalar(out=ckvT[:, cs_, :], in0=ptb[tpk][:, 0:512], scalar1=g_kv[:, 0:1], scalar2=None, op0=ALU.mult),
                         [tptok, t_bias1], [ckvT_toks[cs_][0]])

                def rope_k():
                    x1 = ckr[:, cs_, :, 128:144]
                    x2 = ckr[:, cs_, :, 144:160]
                    csb, t_csb = rope_hold["cs"], rope_hold["tok"]
                    sn = csb[:, 4 * c:4 * c + 4, 0:16]
                    co = csb[:, 4 * c:4 * c + 4, 16:32]
                    rr = ckr_toks[cs_] + [t_csb]
                    P.op("dve", lambda e: e.tensor_tensor(out=rtmp[:, 0], in0=x1, in1=co, op=ALU.mult), rr, [t_rtmp], n=64)
                    P.op("dve", lambda e: e.tensor_tensor(out=rtmp[:, 1], in0=x2, in1=sn, op=ALU.mult), rr, [], [t_rtmp], n=64)
                    P.op("dve", lambda e: e.tensor_tensor(out=rtmp[:, 2], in0=x2, in1=co, op=ALU.mult), rr, [], [t_rtmp], n=64)
                    P.op("dve", lambda e: e.tensor_tensor(out=rtmp[:, 3], in0=x1, in1=sn, op=ALU.mult), rr, [], [t_rtmp], n=64)
                    P.op("dve", lambda e: e.tensor_tensor(out=krot[:, cs_, :, 0:16], in0=rtmp[:, 0], in1=rtmp[:, 1], op=ALU.subtract), [t_rtmp], [krot_toks[cs_]], n=64)
                    P.op("dve", lambda e: e.tensor_tensor(out=krot[:, cs_, :, 16:32], in0=rtmp[:, 2], in1=rtmp[:, 3], op=ALU.add), [t_rtmp], [], [krot_toks[cs_]], n=64)

                def krot_T():
                    tpk, tptok = pt_ring.next()
                    for t in range(4):
                        P.op("pe", lambda e: e.transpose(out=ptb[tpk][0:32, t * 128:(t + 1) * 128], in_=krot[:, cs_, t, :], identity=ident[:]),
                             [krot_toks[cs_], t_ident], [tptok], n=128)
                    P.op("act", lambda e: e.activation(out=krT[:, cs_, :], in_=ptb[tpk][0:32, 0:512], func=AF.Copy), [tptok], [krT_toks[cs_]])
                    for h in range(8):
                        P.dma("pool" if h % 2 else "sp", lambda e: e.dma_start(out=KTm[h, 64:96, cols], in_=krT[:, cs_, :]), [krT_toks[cs_]], [], [t_KTm])

                def fchain():
                    pk, ptok = pa_ring.next()
                    for fc in range(8):
                        P.op("pe", lambda e: e.matmul(pa[pk][0:8, :], lhsT=wK[:, fc, 1024:1032], rhs=hT[:, hs, fc, :], start=(fc == 0), stop=(fc == 7)),
                             hTr + [t_wK], [ptok])
                    P.op("act", lambda e: e.activation(out=fg[:, 0, :], in_=pa[pk][0:8, :], func=AF.Exp, scale=-1.0, bias=nb_f[:, 1:2]), [ptok, t_nbf], [t_fg])
                    P.op("act", lambda e: e.activation(out=fg[:, 1, :], in_=fg[:, 0, :], func=AF.Ln, bias=cst[0:8, 1:2]), [t_fg, t_cst], [t_fg])
                    init = 0.0 if c == 0 else C_all[:, c * 512 - 1:c * 512]
                    P.op("dve", lambda e: e.tensor_tensor_scan(out=C_all[:, cols], data0=fones[:], data1=fg[:, 1, :], initial=init, op0=ALU.mult, op1=ALU.add),
                         [t_fg, t_fones] + ([t_Call[c - 1]] if c else []), [t_Call[c]])
                    fs = c % 2
                    stg = split3(C_all[:, cols], f6[:, fs, 0:3, :], [t_Call[c]], f6_toks[fs], 8.0, fw_[:, fs], t_fw[fs])

                    def fin():
                        stg[2]()
                        P.dma("sp", lambda e: e.dma_start(out=KTf[:, 64:70, cols], in_=f6[:, fs]), [f6_toks[fs]], [], [t_KTf])
                    pend_split.extend([stg[0], stg[1], fin])

                def upproj():
                    for m in range(4):
                        pk, ptok = pa_ring.next()
                        P.op("pe", lambda e: e.matmul(pa[pk][:], lhsT=wuk[:, m * 128:(m + 1) * 128], rhs=ckvT[:, cs_, :], start=True, stop=True),
                             [ckvT_toks[cs_][0], t_wuk], [ptok])
                        sk, stok = knst_ring.next()
                        evac_copy(knst[:, sk, :], pa[pk][:], [ptok], [stok])
                        for hh in range(2):
                            P.dma("pool", lambda e: e.dma_start(out=KTm[2 * m + hh, 0:64, cols], in_=knst[hh * 64:(hh + 1) * 64, sk, :]), [stok], [], [t_KTm])
                    for t in range(4):
                        pk, ptok = pa_ring.next()
                        P.op("pe", lambda e: e.matmul(pa[pk][:], lhsT=ckvT[:, cs_, t * 128:(t + 1) * 128], rhs=wuv[:], start=True, stop=True),
                             [ckvT_toks[cs_][0], t_wuv], [ptok])
                        dst = bass.AP(vmst, vs * 8 * 260 + t * 65, [[2 * 8 * 260, 128], [260, 8], [1, 64]])
                        evac_copy(dst, pa[pk][:].rearrange("p (h d) -> p h d", h=8), [ptok], ([vmst_toks[vs]] if t == 0 else []), cow=([] if t == 0 else [vmst_toks[vs]]))
                    P.dma("pool", lambda e: e.dma_start(out=Vm[:, :, c * 260:(c + 1) * 260].rearrange("h p x -> p h x"), in_=vmst[:, vs]), [vmst_toks[vs]], [], [t_Vm])

                def s0():
                    for t in range(4):
                        ckr_mm(t)
                    kproj(0)
                    kproj(1)

                def s1():
                    kproj(2)
                    kproj(3)
                    ckn_T()
                    rope_k()
                    vproj(0)

                def s2():
                    vproj(1)
                    vproj(2)
                    vproj(3)

                def s3():
                    krot_T()
                    fchain()
                    upproj()
                return [s0, s1, s2, s3]

            p2 = {}

            def p2_setup():
                esA2 = contextlib.ExitStack()
                p2["es"] = esA2
                sb2 = mk_alloc(esA2)
                p2["wQ"] = wQ_s
                p2["wuq"] = wuq_s
                p2["wG"] = sb2("wG", [128, 8, 2048], BF16)
                p2["t_wQ"], p2["t_wuq"], p2["t_wG"] = t_wQ_s, t_wuq_s, Tok()
                wQ, wuq, wG = p2["wQ"], p2["wuq"], p2["wG"]
                for fc in range(8):
                    P.dma("sp", lambda e: e.dma_start(out=wG[:, fc, :], in_=WGb[fc * 128:(fc + 1) * 128, :]), [t_WGb], [], [p2["t_wG"]])
                p2["b_fq"] = sb2("b_fq", [128, 4], F32)
                p2["b_g"] = sb2("b_g", [128, 16], F32)
                p2["b_cq"] = sb2("b_cq", [128, 256], F32)
                p2["g_q"] = sb2("g_q", [128, 2], F32)
                p2["selt"] = sb2("selt", [8, 4], F32)
                p2["t_bias2"] = Tok()
                tb = p2["t_bias2"]
                load_colvec(p2["b_fq"][:], b_in_h, C_FQ, 4, tb)
                load_colvec(p2["b_g"][:], b_in_h, C_GA, 16, tb)
                P.dma("sp", lambda e: e.dma_start(out=p2["b_cq"][:], in_=bcast_rows(b_in_h, C_CQ, 256)), [], [], [tb])
                load_colvec(p2["g_q"][:], q_a_norm_h, 0, 2, tb)
                P.dma("sp", lambda e: e.dma_start(out=p2["selt"][:], in_=sel_h.ap()), [], [], [tb])
                p2["cso"], p2["t_cso"] = rope_table(sb2, "ro", poso_h, 16)
                p2["qst"] = sb2("qst", [128, 4, 512], BF16)
                p2["qst_ring"] = Ring(4)
                p2["cq"] = sb2("cq", [128, 2, 256], F32)
                p2["cq_ring"] = Ring(2)
                p2["cqn"] = sb2("cqn", [128, 4, 256], BF16)
                p2["cqn_toks"] = [Tok() for _ in range(4)]
                p2["cqT"] = sb2("cqT", [128, 2, 2, 512], BF16)
                p2["cqT_toks"] = [Tok(), Tok()]
                p2["qf"] = sb2("qf", [128, 2, 768], F32)
                p2["qf_ring"] = Ring(2)
                p2["qtmp"] = sb2("qtmp", [128, 4, 8, 16], F32)
                p2["t_qtmp"] = Tok()
                p2["qb"] = sb2("qb", [128, 4, 768], BF16)
                p2["qb_toks"] = [Tok() for _ in range(4)]
                p2["qmst"] = sb2("qmst", [96, 8, 512], BF16)
                p2["t_qmst"] = Tok()
                p2["sgst"] = sb2("sgst", [128, 4, 512], BF16)
                p2["sgst_ring"] = Ring(4)
                p2["co"] = sb2("co", [8, 512], F32)
                p2["t_co"] = Tok()
                p2["fw2"] = sb2("fw2", [8, 2, 512], F32)
                p2["t_fw2"] = Tok()
                p2["f6q"] = sb2("f6q", [8, 6, 512], BF16)
                p2["t_f6q"] = Tok()
                P.op("pool", lambda e: e.memset(p2["f6q"][:], 1.0), [], [p2["t_f6q"]])

            def p2_slots(j):
                hs = j % 2
                cols = slice(j * 512, (j + 1) * 512)
                hTr = hT_toks[hs]
                wQ, wuq, wG = p2["wQ"], p2["wuq"], p2["wG"]
                t_wQ, t_wuq, t_wG, tb = p2["t_wQ"], p2["t_wuq"], p2["t_wG"], p2["t_bias2"]
                b_fq, b_g, b_cq, g_q, selt = p2["b_fq"], p2["b_g"], p2["b_cq"], p2["g_q"], p2["selt"]
                cso, t_cso = p2["cso"], p2["t_cso"]
                qst, qst_ring, cq, cq_ring, cqn, cqn_toks = p2["qst"], p2["qst_ring"], p2["cq"], p2["cq_ring"], p2["cqn"], p2["cqn_toks"]
                cqs = j % 2
                cqT, cqT_tok = p2["cqT"], p2["cqT_toks"][cqs]
                qf, qf_ring, qtmp, t_qtmp, qb, qb_toks = p2["qf"], p2["qf_ring"], p2["qtmp"], p2["t_qtmp"], p2["qb"], p2["qb_toks"]
                qmst, t_qmst, sgst, sgst_ring = p2["qmst"], p2["t_qmst"], p2["sgst"], p2["sgst_ring"]
                co_, t_co, fw2, t_fw2, f6q, t_f6q = p2["co"], p2["t_co"], p2["fw2"], p2["t_fw2"], p2["f6q"], p2["t_f6q"]

                def qfox(m):
                    pk, ptok = pa_ring.next()
                    for fc in range(8):
                        P.op("pe", lambda e: e.matmul(pa[pk][:], lhsT=wQ[:, fc, m * 128:(m + 1) * 128], rhs=hT[:, hs, fc, :],
                                                      start=(fc == 0), stop=(fc == 7)), hTr + [t_wQ], [ptok])
                    sk, stok = qst_ring.next()
                    P.op("act", lambda e: e.activation(out=qst[:, sk, :], in_=pa[pk][:], func=AF.Identity, bias=b_fq[:, m:m + 1]), [ptok, tb], [stok])
                    for hh in range(2):
                        P.dma("sp", lambda e: e.dma_start(out=QTf[2 * m + hh, 0:64, cols], in_=qst[hh * 64:(hh + 1) * 64, sk, :]), [stok], [], [t_QTf])

                def ownF():
                    co3 = co_[:].rearrange("p (i r) -> p i r", i=4)
                    rC = [t_Call[4 * j + q] for q in range(4)]
                    for cc in range(4):
                        src = bass.AP(C_all, (16 * j + cc) * 128, [[S, 8], [4 * 128, 4], [1, 128]])
                        if cc == 0:
                            P.op("dve", lambda e: e.tensor_scalar(out=co3, in0=src, scalar1=selt[:, 0:1], scalar2=None, op0=ALU.mult),
                                 rC + [tb], [t_co])
                        else:
                            P.op("dve", lambda e: e.scalar_tensor_tensor(out=co3, in0=src, scalar=selt[:, cc:cc + 1], in1=co3,
                                                                        op0=ALU.mult, op1=ALU.add), rC + [tb, t_co], [t_co])
                    stg = split3(co_[:], f6q[:, 3:6, :], [t_co], t_f6q, -8.0, fw2, t_fw2)

                    def fin():
                        stg[2]()
                        P.dma("sp", lambda e: e.dma_start(out=QTf[:, 64:70, cols], in_=f6q[:]), [t_f6q], [], [t_QTf])
                    pend_split.extend([stg[0], stg[1], fin])

                def cq_mm(t):
                    pk, ptok = pa_ring.next()
                    for fc in range(8):
                        P.op("pe", lambda e: e.matmul(pa[pk][:, 0:256], lhsT=hT[:, hs, fc, t * 128:(t + 1) * 128], rhs=wQ[:, fc, 512:768],
                                                      start=(fc == 0), stop=(fc == 7)), [hTr[t], t_wQ], [ptok], n=256)
                    ck, ctok = cq_ring.next()
                    P.op("dve", lambda e: e.tensor_tensor(out=cq[:, ck, :], in0=pa[pk][:, 0:256], in1=b_cq[:], op=ALU.add), [ptok, tb], [ctok], n=256)
                    st, st_tok = new_stat()
                    P.op("act", lambda e: e.activation(out=junk[:, 0:256], in_=cq[:, ck, :], func=AF.Square, accum_out=st[:, 0:1]), [ctok], [st_tok], n=256)
                    rstd_from_ss(st[:, 0:1], 256, [st_tok], st[:, 2:4], st_tok)
                    P.op("dve", lambda e: e.tensor_scalar(out=cqn[:, t, :], in0=cq[:, ck, :], scalar1=st[:, 2:3], scalar2=None, op0=ALU.mult),
                         [ctok, st_tok], [cqn_toks[t]], n=256)

                def cq_T():
                    for half in range(2):
                        tpk, tptok = pt_ring.next()
                        for tt in range(2):
                            t = half * 2 + tt
                            for lc in range(2):
                                P.op("pe", lambda e: e.transpose(out=ptb[tpk][:, (tt * 2 + lc) * 128:(tt * 2 + lc + 1) * 128], in_=cqn[:, t, lc * 128:(lc + 1) * 128], identity=ident[:]),
                                     [cqn_toks[t], t_ident], [tptok], n=128)
                        for lc in range(2):
                            src = bass.AP(ptb[tpk], lc * 128, [[1024, 128], [256, 2], [1, 128]])
                            dst = bass.AP(cqT, cqs * 1024 + lc * 512 + half * 256, [[2048, 128], [128, 2], [1, 128]])
                            first = (half == 0 and lc == 0)
                            P.op("dve", lambda e: e.tensor_scalar(out=dst, in0=src, scalar1=g_q[:, lc:lc + 1], scalar2=None, op0=ALU.mult),
                                 [tptok, tb], ([cqT_tok] if first else []), ([] if first else [cqT_tok]), n=256)

                def qfull(t):
                    n = 4 * j + t
                    pk, ptok = pa_ring.next()
                    pk2, ptok2 = pa_ring.next()
                    for lc in range(2):
                        P.op("pe", lambda e: e.matmul(pa[pk][:], lhsT=cqT[:, cqs, lc, t * 128:(t + 1) * 128], rhs=wuq[:, lc, 0:512],
                                                      start=(lc == 0), stop=(lc == 1)), [cqT_tok, t_wuq], [ptok])
                    for lc in range(2):
                        P.op("pe", lambda e: e.matmul(pa[pk2][:, 0:256], lhsT=cqT[:, cqs, lc, t * 128:(t + 1) * 128], rhs=wuq[:, lc, 512:768],
                                                      start=(lc == 0), stop=(lc == 1)), [cqT_tok, t_wuq], [ptok2], n=256)
                    fk_, ftok = qf_ring.next()
                    evac_copy(qf[:, fk_, 0:512], pa[pk][:], [ptok], [ftok], eng="act")
                    evac_copy(qf[:, fk_, 512:768], pa[pk2][:, 0:256], [ptok2], [], eng="act", cow=[ftok], n=256)
                    btok = qb_toks[t]
                    q3 = qf[:, fk_, :].rearrange("p (h d) -> p h d", h=8)
                    qb3 = qb[:, t, :].rearrange("p (h d) -> p h d", h=8)
                    x1 = q3[:, :, 64:80]
                    x2 = q3[:, :, 80:96]
                    sn = bass.AP(cso, n * 32, [[16 * 32, 128], [0, 8], [1, 16]])
                    co = bass.AP(cso, n * 32 + 16, [[16 * 32, 128], [0, 8], [1, 16]])
                    rr = [ftok, t_cso]
                    P.op("dve", lambda e: e.tensor_tensor(out=qtmp[:, 0], in0=x1, in1=co, op=ALU.mult), rr, [t_qtmp], n=128)
                    P.op("dve", lambda e: e.tensor_tensor(out=qtmp[:, 1], in0=x2, in1=sn, op=ALU.mult), rr, [], [t_qtmp], n=128)
                    P.op("dve", lambda e: e.tensor_tensor(out=qtmp[:, 2], in0=x2, in1=co, op=ALU.mult), rr, [], [t_qtmp], n=128)
                    P.op("dve", lambda e: e.tensor_tensor(out=qtmp[:, 3], in0=x1, in1=sn, op=ALU.mult), rr, [], [t_qtmp], n=128)
                    P.op("dve", lambda e: e.tensor_tensor(out=qb3[:, :, 64:80], in0=qtmp[:, 0], in1=qtmp[:, 1], op=ALU.subtract), [t_qtmp], [btok], n=128)
                    P.op("dve", lambda e: e.tensor_tensor(out=qb3[:, :, 80:96], in0=qtmp[:, 2], in1=qtmp[:, 3], op=ALU.add), [t_qtmp], [], [btok], n=128)
                    P.op("act", lambda e: e.activation(out=qb3[:, :, 0:64], in_=q3[:, :, 0:64], func=AF.Copy), [ftok], [], [btok])

                def q_T():
                    for t in range(4):
                        tpk, tptok = pt_ring.next()
                        for h in range(8):
                            P.op("pe", lambda e: e.transpose(out=ptb[tpk][0:96, h * 128:(h + 1) * 128], in_=qb[:, t, h * 96:(h + 1) * 96], identity=ident[:]),
                                 [qb_toks[t], t_ident], [tptok], n=128)
                        evac_copy(qmst[:, :, t * 128:(t + 1) * 128], ptb[tpk][0:96, :].rearrange("p (h t) -> p h t", h=8), [tptok],
                                  ([t_qmst] if t == 0 else []), cow=([] if t == 0 else [t_qmst]), n=1024)
                    P.dma("sp", lambda e: e.dma_start(out=QTm[:, :, cols].rearrange("h r c -> r h c"), in_=qmst[:]), [t_qmst], [], [t_QTm])

                def gate(gi):
                    mixer, oc = gi // 8, gi % 8
                    pk, ptok = pa_ring.next()
                    for fc in range(8):
                        P.op("pe", lambda e: e.matmul(pa[pk][:], lhsT=wG[:, fc, gi * 128:(gi + 1) * 128], rhs=hT[:, hs, fc, :],
                                                      start=(fc == 0), stop=(fc == 7)), hTr + [t_wG], [ptok])
                    sk, stok = sgst_ring.next()
                    P.op("act", lambda e: e.activation(out=sgst[:, sk, :], in_=pa[pk][:], func=AF.Sigmoid, bias=b_g[:, gi:gi + 1]), [ptok, tb], [stok])
                    P.dma("pool" if gi % 2 else "sp", lambda e: e.dma_start(out=SG[mixer, oc, :, cols], in_=sgst[:, sk, :]), [stok], [], [t_SG])

                def s0():
                    for t in range(4):
                        cq_mm(t)
                    for m in range(4):
                        qfox(m)
                    ownF()

                def s1():
                    cq_T()
                    for gi in range(0, 5):
                        gate(gi)

                def s2():
                    for t in range(4):
                        qfull(t)
                    for gi in range(5, 11):
                        gate(gi)

                def s3():
                    q_T()
                    for gi in range(11, 16):
                        gate(gi)
                return [s0, s1, s2, s3]

            NCH = 20
            for t in range(4):
                sA_normA(0, t)
                sA_normB(0, t)
            sA_load(1)
            rope_hold["cs"], rope_hold["tok"] = rope_table(sb1, "rb", posb_h, 64)
            for ci in range(NCH):
                slots = p1_slots(ci) if ci < 16 else p2_slots(ci - 16)
                for t in range(4):
                    if ci + 1 < NCH:
                        sA_normA(ci + 1, t)
                    drain_split(1)
                    slots[t]()
                    if ci + 1 < NCH:
                        sA_normB(ci + 1, t)
                if ci + 2 < NCH:
                    sA_load(ci + 2)
                if 1 <= ci <= 9:
                    emit_p2_prefetch(3)
                if ci == 15 or ci == NCH - 1:
                    drain_split(8)
                if ci == 15:
                    P.barrier()
                    esA1.close()
                    p2_setup()
            P.barrier()
            p2["es"].close()

        with contextlib.ExitStack() as esC2:
            sbC2 = mk_alloc(esC2)
            wff1 = sbC2("wff1", [128, 8, DFF], BF16)
            t_wff1 = Tok()
            with contextlib.ExitStack() as esC1:
                sbC1 = mk_alloc(esC1)
                wom = sbC1("wom", [128, 4, D], BF16)
                wof = sbC1("wof", [128, 4, D], BF16)
                wout = sbC1("wout", [128, 8, D], BF16)
                t_wom, t_wof, t_wout = Tok(), Tok(), Tok()

                w3_pref = []
                for kc in range(4):
                    w3_pref.append((lambda e, kc=kc: e.dma_start(out=wom[:, kc, :], in_=w_om_h.ap()[kc * 128:(kc + 1) * 128, :]), t_wom))
                    w3_pref.append((lambda e, kc=kc: e.dma_start(out=wof[:, kc, :], in_=w_of_h.ap()[kc * 128:(kc + 1) * 128, :]), t_wof))
                for kc in range(8):
                    w3_pref.append((lambda e, kc=kc: e.dma_start(out=wout[:, kc, :], in_=w_out_h.ap()[kc * 128:(kc + 1) * 128, :]), t_wout))
                for kc in range(8):
                    for hf in range(2):
                        w3_pref.append((lambda e, kc=kc, hf=hf: e.dma_start(out=wff1[:, kc, hf * 2048:(hf + 1) * 2048],
                                                                            in_=w_ff1_h.ap()[kc * 128:(kc + 1) * 128, hf * 2048:(hf + 1) * 2048]), t_wff1))
                for kc in range(32):
                    w3_pref.append((lambda e, kc=kc: e.dma_start(out=FF2b[kc * 128:(kc + 1) * 128, :], in_=w_ff2_h.ap()[kc * 128:(kc + 1) * 128, :]), t_FF2b))

                def emit_w3_loads(k):
                    for _ in range(k):
                        if w3_pref:
                            fn_, tk_ = w3_pref.pop(0)
                            P.dma("pool", fn_, [], [], [tk_])

                with contextlib.ExitStack() as esB:
                    sbB = mk_alloc(esB)
                    SP = [esB.enter_context(nc.psum_tensor(f"sp{k}", [128, 1024], F32)) for k in range(3)]
                    OA = [esB.enter_context(nc.psum_tensor(f"oa{k}", [128, 512], F32)) for k in range(2)]
                    KT = sbB("KT", [96, 2, S], BF16)
                    VV = sbB("VV", [128, 2, 64 * 65], BF16)
                    QQ = sbB("QQ", [96, 2, NOWN], BF16)
                    kvq_toks = [[Tok(), Tok(), Tok()] for _ in range(2)]
                    maskT = sbB("maskTs", [128, 16, 128], BF16)
                    t_mask = Tok()
                    P.dma("sp", lambda e: e.dma_start(out=maskT[:].rearrange("p a b -> p (a b)"), in_=mask_h.ap()), [], [t_mask])
                    NPT = 4
                    PT = sbB("PT", [128, NPT, 1024], BF16)
                    PT_ring = Ring(NPT)
                    osb = sbB("osb", [65, 2, 512], F32)
                    osb_ring = Ring(2)
                    rcp = sbB("rcp", [65, 2, 512], F32)
                    rcp_ring = Ring(2)
                    ost = sbB("ost", [64, 2, 512], BF16)
                    ost_ring = Ring(2)
                    SP_ring = Ring(3)
                    O_ring = Ring(2)
                    units = [(0, h) for h in range(8)] + [(1, h) for h in range(8)]

                    def emit_loads(u):
                        mixer, h = units[u]
                        s_ = u % 2
                        dk = 96 if mixer == 0 else 70
                        Ksrc, Vsrc, Qsrc = (KTm, Vm, QTm) if mixer == 0 else (KTf, Vf, QTf)
                        tK, tV, tQ = kvq_toks[s_]
                        rK = [t_KTm] if mixer == 0 else [t_KTf]
                        rV = [t_Vm] if mixer == 0 else [t_Vf]
                        rQ = [t_QTm] if mixer == 0 else [t_QTf]
                        P.dma("sp", lambda e: e.dma_start(out=QQ[0:dk, s_, :], in_=Qsrc[h, 0:dk, :]), rQ, [tQ])
                        for q4 in range(4):
                            P.dma("sp", lambda e: e.dma_start(out=KT[0:dk, s_, q4 * 2048:(q4 + 1) * 2048], in_=Ksrc[h, 0:dk, q4 * 2048:(q4 + 1) * 2048]), rK,
                                  ([tK] if q4 == 0 else []), ([] if q4 == 0 else [tK]))
                        for v2 in range(2):
                            P.dma("sp", lambda e: e.dma_start(out=VV[:, s_, v2 * 2080:(v2 + 1) * 2080], in_=Vsrc[h, :, v2 * 2080:(v2 + 1) * 2080]), rV,
                                  ([tV] if v2 == 0 else []), ([] if v2 == 0 else [tV]))

                    deferred = []

                    def finalize(u, j, ok, otok):
                        mixer, h = units[u]
                        sk, stok = osb_ring.next()
                        P.op("dve", lambda e: e.tensor_copy(out=osb[:, sk, :], in_=OA[ok][0:65, :]), [otok], [stok])
                        rk, rtok = rcp_ring.next()
                        P.op("dve", lambda e: e.reciprocal(out=rcp[64:65, rk, :], in_=osb[64:65, sk, :]), [stok], [rtok])

                        def pe_part():
                            P.op("pe", lambda e: e.matmul(OA[ok][0:64, :], lhsT=onesf[64:65, 0:64], rhs=rcp[64:65, rk, :], start=True, stop=True), [rtok, t_onesf], [otok])
                            tk_, ttok = ost_ring.next()
                            P.op("dve", lambda e: e.tensor_tensor(out=ost[:, tk_, :], in0=osb[0:64, sk, :], in1=OA[ok][0:64, :], op=ALU.mult), [stok, otok], [ttok])
                            P.dma("sp", lambda e: e.dma_start(out=OT[mixer, h * 64:(h + 1) * 64, j * 512:(j + 1) * 512], in_=ost[:, tk_, :]), [ttok], [], [t_OT])
                        deferred.append(pe_part)

                    def slot_groups(j):
                        groups = []
                        for p in range(8 * j):
                            groups.append([(2 * p, 0, 512, None, 0), (2 * p + 1, 0, 512, None, 512)])
                        base = 16 * j
                        for w0 in (0, 2):
                            groups.append([(base + w0, 0, 512, w0, 0), (base + w0 + 1, 0, 512, w0 + 1, 512)])
                        for w0 in (4, 6):
                            groups.append([(base + w0, 128, 384, w0, 0), (base + w0 + 1, 128, 384, w0 + 1, 512)])
                        groups.append([(base + 8 + q, 256, 256, 8 + q, 256 * q) for q in range(4)])
                        groups.append([(base + 12 + q, 384, 128, 12 + q, 128 * q) for q in range(4)])
                        return groups

                    LAG = 2
                    emit_loads(0)
                    glist = []
                    for u in range(16):
                        for j in range(4):
                            groups = slot_groups(j)
                            for gi, grp in enumerate(groups):
                                glist.append((u, j, grp, gi == 0, gi == len(groups) - 1))
                    infl = []
                    slot_state = {}
                    pend = []
                    for idx in range(len(glist) + LAG):
                        if idx == 6:
                            emit_loads(1)
                        if idx < len(glist):
                            u, j, grp, first, last = glist[idx]
                            mixer, h = units[u]
                            s_ = u % 2
                            dk = 96 if mixer == 0 else 70
                            scale = float(96 ** -0.5) if mixer == 0 else 0.125
                            tK, tV, tQ = kvq_toks[s_]
                            sp, sptok = SP_ring.next()
                            for bi, (kb, c0, N, w, off) in enumerate(grp):
                                win = w is not None
                                P.op("pe", lambda e: e.matmul(SP[sp][:, off:off + N], lhsT=KT[0:dk, s_, kb * 128:(kb + 1) * 128],
                                                              rhs=QQ[0:dk, s_, j * 512 + c0:(j + 1) * 512], start=True, stop=(not win)),
                                     [tK, tQ], ([sptok] if bi == 0 else []), ([] if bi == 0 else [sptok]))
                                if win:
                                    P.op("pe", lambda e: e.matmul(SP[sp][:, off:off + 128], lhsT=ident[:], rhs=maskT[:, w, :],
                                                                  start=False, stop=True), [t_ident, t_mask], [], [sptok])
                            pt_, pttok = PT_ring.next()
                            N0 = grp[0][2]
                            if len(grp) == 2 and N0 < 512:
                                src = SP[sp][:].rearrange("p (a b) -> p a b", a=2)[:, :, 0:N0]
                                dst = PT[:, pt_, :].rearrange("p (a b) -> p a b", a=2)[:, :, 0:N0]
                            else:
                                tot = N0 * len(grp)
                                src = SP[sp][:, 0:tot]
                                dst = PT[:, pt_, 0:tot]
                            P.op("act", lambda e: e.activation(out=dst, in_=src, func=AF.Exp, scale=scale), [sptok], [pttok])
                            infl.append((u, j, grp, pt_, pttok, first, last))
                        if idx >= LAG:
                            u, j, grp, pt_, pttok, first, last = infl[idx - LAG]
                            s_ = u % 2
                            tK, tV, tQ = kvq_toks[s_]
                            if first:
                                slot_state[(u, j)] = O_ring.next()
                            ok, otok = slot_state[(u, j)]
                            last_kb = 16 * j + 15
                            for (kb, c0, N, w, off) in grp:
                                P.op("pe", lambda e: e.matmul(OA[ok][0:65, c0:512], lhsT=VV[:, s_, kb * 65:(kb + 1) * 65], rhs=PT[:, pt_, off:off + N],
                                                              start=(kb == 0), stop=(kb == last_kb)), [pttok, tV], [otok])
                            if last:
                                finalize(u, j, ok, otok)
                                pend.append((idx + 5, deferred.pop()))
                                if j == 3 and u + 2 < 16:
                                    emit_loads(u + 2)
                                if u >= 1:
                                    emit_w3_loads(2)
                        while pend and pend[0][0] <= idx:
                            pend.pop(0)[1]()
                    while pend:
                        pend.pop(0)[1]()
                    emit_w3_loads(1000)
                    P.barrier()

                with contextlib.ExitStack() as es3a:
                    sb3 = mk_alloc(es3a)
                    pa = [es3a.enter_context(nc.psum_tensor(f"pc{k}", [128, 512], F32)) for k in range(6)]
                    ptb = [es3a.enter_context(nc.psum_tensor(f"pu{k}", [128, 1024], BF16)) for k in range(2)]
                    pa_ring = Ring(6)
                    pt_ring = Ring(2)
                    otm = sb3("otm", [128, 2, 2, 4, 512], BF16)
                    otm_toks = [Tok(), Tok()]
                    sgt = sb3("sgt", [128, 4, 2, 512], BF16)
                    sgt_ring = Ring(4)
                    mtmp = sb3("mtmp", [128, 2, 2, 512], F32)
                    mtmp_ring = Ring(2)
                    mgT = sb3("mgT", [128, 2, 8, 512], BF16)
                    mgT_toks = [[Tok() for _ in range(8)] for _ in range(2)]
                    xo_t = sb3("xo_t", [128, 2, D], F32)
                    xo_ring = Ring(2)
                    x1t = sb3("x1t", [128, 2, D], F32)
                    x1_ring = Ring(2)
                    h2n = sb3("h2n", [128, 3, D], BF16)
                    h2n_ring = Ring(3)
                    h2st = sb3("h2st", [128, 2, 8, 512], BF16)
                    h2st_toks = [[Tok() for _ in range(4)] for _ in range(2)]
                    sg_loaded = {}
                    gbcB, t_gbcB = load_gains(sb3, "gbcB", [1, 2])

                    def load_ot(j):
                        cols = slice(j * 512, (j + 1) * 512)
                        for mixer in range(2):
                            P.dma("sp", lambda e: e.dma_start(out=otm[:, j % 2, mixer], in_=OT[mixer, :, cols].rearrange("(kc p) t -> p kc t", p=128)),
                                  [t_OT], ([otm_toks[j % 2]] if mixer == 0 else []), ([] if mixer == 0 else [otm_toks[j % 2]]))

                    def load_sg(j, oc):
                        if j >= 4:
                            return
                        cols = slice(j * 512, (j + 1) * 512)
                        gk, gtok = sgt_ring.next()
                        for mixer in range(2):
                            P.dma("sp", lambda e: e.dma_start(out=sgt[:, gk, mixer, :], in_=SG[mixer, oc, :, cols]), [t_SG],
                                  ([gtok] if mixer == 0 else []), ([] if mixer == 0 else [gtok]))
                        sg_loaded[(j, oc)] = (gk, gtok)

                    def Y(j, oc):
                        if j >= 4:
                            return
                        os_ = j % 2
                        nxt = j * 8 + oc + 2
                        load_sg(nxt // 8, nxt % 8)
                        gk, gtok = sg_loaded[(j, oc)]
                        pk, ptok = pa_ring.next()
                        pk2, ptok2 = pa_ring.next()
                        for kc in range(4):
                            P.op("pe", lambda e: e.matmul(pa[pk][:], lhsT=wom[:, kc, oc * 128:(oc + 1) * 128], rhs=otm[:, os_, 0, kc, :],
                                                          start=(kc == 0), stop=(kc == 3)), [otm_toks[os_], t_wom], [ptok])
                        for kc in range(4):
                            P.op("pe", lambda e: e.matmul(pa[pk2][:], lhsT=wof[:, kc, oc * 128:(oc + 1) * 128], rhs=otm[:, os_, 1, kc, :],
                                                          start=(kc == 0), stop=(kc == 3)), [otm_toks[os_], t_wof], [ptok2])
                        mk_, mtok = mtmp_ring.next()
                        P.op("dve", lambda e: e.tensor_tensor(out=mtmp[:, mk_, 0, :], in0=pa[pk][:], in1=sgt[:, gk, 0, :], op=ALU.mult), [ptok, gtok], [mtok])
                        P.op("dve", lambda e: e.tensor_tensor(out=mtmp[:, mk_, 1, :], in0=pa[pk2][:], in1=sgt[:, gk, 1, :], op=ALU.mult), [ptok2, gtok], [], [mtok])
                        P.op("pool", lambda e: e.tensor_tensor(out=mgT[:, os_, oc, :], in0=mtmp[:, mk_, 0, :], in1=mtmp[:, mk_, 1, :], op=ALU.add),
                             [mtok], [mgT_toks[os_][oc]])

                    pendB = {}

                    def W(j, t):
                        os_ = j % 2
                        row0 = j * 512 + t * 128
                        xk, xtok = xo_ring.next()
                        P.dma("sp", lambda e: e.dma_start(out=xo_t[:, xk, :], in_=xo[row0:row0 + 128, :]), [], [xtok])
                        pks = []
                        for hf in range(2):
                            pk, ptok = pa_ring.next()
                            for kc in range(8):
                                P.op("pe", lambda e: e.matmul(pa[pk][:], lhsT=mgT[:, os_, kc, t * 128:(t + 1) * 128], rhs=wout[:, kc, hf * 512:(hf + 1) * 512],
                                                              start=(kc == 0), stop=(kc == 7)), [mgT_toks[os_][kc], t_wout], [ptok])
                            pks.append((pk, ptok))
                        st, st_tok = new_stat()
                        for hf in range(2):
                            pk, ptok = pks[hf]
                            P.op("act", lambda e: e.activation(out=junk[:, 0:512], in_=pa[pk][:], func=AF.Square, accum_out=st[:, hf:hf + 1]),
                                 [ptok], ([st_tok] if hf == 0 else []), ([] if hf == 0 else [st_tok]))
                        P.op("dve", lambda e: e.tensor_tensor(out=st[:, 0:1], in0=st[:, 0:1], in1=st[:, 1:2], op=ALU.add), [st_tok], [st_tok], n=1)
                        rstd_from_ss(st[:, 0:1], D, [st_tok], st[:, 2:4], st_tok)
                        x1k, x1tok = x1_ring.next()
                        for hf in range(2):
                            pk, ptok = pks[hf]
                            P.op("dve", lambda e: e.scalar_tensor_tensor(out=x1t[:, x1k, hf * 512:(hf + 1) * 512], in0=pa[pk][:], scalar=st[:, 2:3],
                                                                        in1=gbcB[:, 0, hf * 512:(hf + 1) * 512], op0=ALU.mult, op1=ALU.mult),
                                 [ptok, st_tok, t_gbcB], ([x1tok] if hf == 0 else []), ([] if hf == 0 else [x1tok]))
                        P.op("pool", lambda e: e.tensor_tensor(out=x1t[:, x1k, :], in0=x1t[:, x1k, :], in1=xo_t[:, xk, :], op=ALU.add), [x1tok, xtok], [x1tok])
                        P.dma("sp", lambda e: e.dma_start(out=X1[row0:row0 + 128, :], in_=x1t[:, x1k, :]), [x1tok], [], [t_X1])
                        pendB[(j, t)] = normA(x1t[:, x1k, :], x1tok, gbcB[:, 1, :], t_gbcB, h2n, h2n_ring)

                    def WB(j, t):
                        os_ = j % 2
                        hk, htok = pendB[(j, t)]
                        normB(h2n, hk, htok, h2st[:, os_, :, t * 128:(t + 1) * 128], h2st_toks[os_][t], ptb, pt_ring)
                        if t == 3:
                            P.dma("sp", lambda e: e.dma_start(out=H2T[j].rearrange("p (c t) -> p c t", c=8), in_=h2st[:, os_]), h2st_toks[os_], [], [t_H2T])

                    load_ot(0)
                    load_sg(0, 0)
                    load_sg(0, 1)
                    for oc in range(8):
                        Y(0, oc)
                    for j in range(4):
                        if j + 1 < 4:
                            load_ot(j + 1)
                        for t in range(4):
                            W(j, t)
                            Y(j + 1, 2 * t)
                            Y(j + 1, 2 * t + 1)
                            WB(j, t)
                    P.barrier()

            with contextlib.ExitStack() as es3b:
                sb4 = mk_alloc(es3b)
                pa = [es3b.enter_context(nc.psum_tensor(f"pd{k}", [128, 512], F32)) for k in range(6)]
                pa_ring = Ring(6)
                wff2 = sb4("wff2", [128, 32, D], BF16)
                t_wff2 = Tok()
                h2T = sb4("h2T", [128, 8, 512], BF16)
                t_h2T = Tok()
                aT = sb4("aT", [128, 32, 512], BF16)
                aT_toks = [Tok() for _ in range(32)]
                rl = sb4("rl", [128, 2, 512], BF16)
                rl_ring = Ring(2)
                x1l = sb4("x1l", [128, 2, D], F32)
                x1l_ring = Ring(2)
                yo = sb4("yo", [128, 2, D], F32)
                yo_ring = Ring(2)

                gbcC, t_gbcC = load_gains(sb4, "gbcC", [3])

                def load_h2(j):
                    P.dma("sp", lambda e: e.dma_start(out=h2T[:], in_=H2T[j].rearrange("p (c t) -> p c t", c=8)), [t_H2T], [t_h2T])

                def load_x1(row0):
                    xk, xtok = x1l_ring.next()
                    P.dma("sp", lambda e: e.dma_start(out=x1l[:, xk, :], in_=X1[row0:row0 + 128, :]), [t_X1], [xtok])
                    return xk, xtok

                load_h2(0)
                for kc in range(32):
                    P.dma("sp", lambda e: e.dma_start(out=wff2[:, kc, :], in_=FF2b[kc * 128:(kc + 1) * 128, :]), [t_FF2b], [], [t_wff2])
                for j in range(4):
                    for oc in range(32):
                        pk, ptok = pa_ring.next()
                        for fc in range(8):
                            P.op("pe", lambda e: e.matmul(pa[pk][:], lhsT=wff1[:, fc, oc * 128:(oc + 1) * 128], rhs=h2T[:, fc, :],
                                                          start=(fc == 0), stop=(fc == 7)), [t_h2T, t_wff1], [ptok])
                        rk, rtok = rl_ring.next()
                        P.op("act", lambda e: e.activation(out=rl[:, rk, :], in_=pa[pk][:], func=AF.Relu), [ptok], [rtok])
                        P.op("pool" if oc % 2 else "dve", lambda e: e.tensor_tensor(out=aT[:, oc, :], in0=rl[:, rk, :], in1=rl[:, rk, :], op=ALU.mult), [rtok], [aT_toks[oc]])
                    if j + 1 < 4:
                        load_h2(j + 1)
                    nxt_x = load_x1(j * 512)
                    for t in range(4):
                        row0 = j * 512 + t * 128
                        xk, xtok = nxt_x
                        if t + 1 < 4:
                            nxt_x = load_x1(row0 + 128)
                        pks = []
                        for hf in range(2):
                            pk, ptok = pa_ring.next()
                            for kc in range(32):
                                P.op("pe", lambda e: e.matmul(pa[pk][:], lhsT=aT[:, kc, t * 128:(t + 1) * 128], rhs=wff2[:, kc, hf * 512:(hf + 1) * 512],
                                                              start=(kc == 0), stop=(kc == 31)), [aT_toks[kc], t_wff2], [ptok])
                            pks.append((pk, ptok))
                        st, st_tok = new_stat()
                        for hf in range(2):
                            pk, ptok = pks[hf]
                            P.op("act", lambda e: e.activation(out=junk[:, 0:512], in_=pa[pk][:], func=AF.Square, accum_out=st[:, hf:hf + 1]),
                                 [ptok], ([st_tok] if hf == 0 else []), ([] if hf == 0 else [st_tok]))
                        P.op("dve", lambda e: e.tensor_tensor(out=st[:, 0:1], in0=st[:, 0:1], in1=st[:, 1:2], op=ALU.add), [st_tok], [st_tok], n=1)
                        rstd_from_ss(st[:, 0:1], D, [st_tok], st[:, 2:4], st_tok)
                        yk, ytok = yo_ring.next()
                        for hf in range(2):
                            pk, ptok = pks[hf]
                            P.op("dve", lambda e: e.scalar_tensor_tensor(out=yo[:, yk, hf * 512:(hf + 1) * 512], in0=pa[pk][:], scalar=st[:, 2:3],
                                                                        in1=gbcC[:, 0, hf * 512:(hf + 1) * 512], op0=ALU.mult, op1=ALU.mult),
                                 [ptok, st_tok, t_gbcC], ([ytok] if hf == 0 else []), ([] if hf == 0 else [ytok]))
                        P.op("pool", lambda e: e.tensor_tensor(out=yo[:, yk, :], in0=yo[:, yk, :], in1=x1l[:, xk, :], op=ALU.add), [ytok, xtok], [ytok])
                        P.dma("sp", lambda e: e.dma_start(out=outp[row0:row0 + 128, :], in_=yo[:, yk, :]), [ytok], [], [t_out])
                P.op("sp", lambda e: None, [t_out], [], track=False)

                P.reschedule(RESCHED_SEGS)
                run_engine = P.emit(sems)
                with nc.Block() as block:
                    @block.sync
                    def _(e):
                        run_engine("sp", e)

                    @block.tensor
                    def _(e):
                        run_engine("pe", e)

                    @block.scalar
                    def _(e):
                        run_engine("act", e)

                    @block.vector
                    def _(e):
                        run_engine("dve", e)

                    @block.gpsimd
                    def _(e):
                        run_engine("pool", e)
    return nc


def _host_consts(g):
    w = np.arange(16)
    kpos = w[:, None] * 128 + np.arange(128)[None, :]
    qpos = ((4 * (w // 4) + g) * 128)[:, None] + np.arange(128)[None, :]
    m = np.where(kpos[:, :, None] <= qpos[:, None, :], 0.0, MASKV).astype(np.float32)
    maskT = np.ascontiguousarray(m.transpose(1, 0, 2).reshape(128, 16 * 128)).astype(ml_dtypes.bfloat16)
    invf = (10000.0 ** (-np.arange(16, dtype=np.float32) / 16.0)).astype(np.float32)
    rc = np.concatenate([invf, invf, np.zeros(16, np.float32), np.full(16, np.pi / 2, np.float32)])
    ropec = np.ascontiguousarray(np.broadcast_to(rc[None, :], (128, 64))).astype(np.float32)
    sel = np.zeros((8, 4), np.float32)
    sel[:, g] = 1.0
    return maskT, ropec, sel


_NC_CACHE = {}


def kernel(x, positions, ln_pre_mix, ln_post_mix, ln_pre_mlp, ln_post_mlp, w_in, b_in, q_a_norm, w_uq,
           kv_a_norm, w_uk, w_uv, w_o_mla, w_o_fox, w_out, w_ff1, w_ff2, _dbg=False):
    x = np.asarray(x, np.float32)
    positions = np.asarray(positions, np.int32)
    if _dbg not in _NC_CACHE:
        _NC_CACHE[_dbg] = build_program(dbg=_dbg)
    nc = _NC_CACHE[_dbg]
    f32 = lambda a: np.ascontiguousarray(np.asarray(a, np.float32))
    shared = {
        "w_in": f32(w_in[0]), "b_in": f32(b_in[0]),
        "ln_pre_mix": f32(ln_pre_mix[0]), "ln_post_mix": f32(ln_post_mix[0]),
        "ln_pre_mlp": f32(ln_pre_mlp[0]), "ln_post_mlp": f32(ln_post_mlp[0]),
        "q_a_norm": f32(q_a_norm[0]), "kv_a_norm": f32(kv_a_norm[0]),
        "w_uq": f32(w_uq[0]), "w_uk": f32(w_uk[0]), "w_uv": f32(w_uv[0]),
        "w_o_mla": f32(w_o_mla[0]), "w_o_fox": f32(w_o_fox[0]), "w_out": f32(w_out[0]),
        "w_ff1": f32(w_ff1[0]), "w_ff2": f32(w_ff2[0]),
    }
    in_maps = []
    own_idx = []
    for c in range(8):
        b, g = c // 4, c % 4
        idx = np.concatenate([np.arange((4 * m_ + g) * 128, (4 * m_ + g + 1) * 128) for m_ in range(16)])
        own_idx.append((b, idx))
        maskT, ropec, sel = _host_consts(g)
        m = dict(shared)
        m["xb"] = np.ascontiguousarray(x[b])
        m["xo"] = np.ascontiguousarray(x[b][idx])
        m["posb"] = np.ascontiguousarray(positions[b].reshape(64, 128).T)
        m["poso"] = np.ascontiguousarray(positions[b][idx].reshape(16, 128).T)
        m["sel"] = sel
        m["maskT"] = maskT
        m["ropec"] = ropec
        in_maps.append(m)
    res = run_bass_kernel_spmd(nc, in_maps, core_ids=list(range(8)))
    out = np.empty((2, S, D), np.float32)
    for c in range(8):
        b, idx = own_idx[c]
        out[b, idx] = np.asarray(res.results[c]["out"], np.float32)
    if _dbg:
        return out, res.results
    return out
```

```python
import contextlib
import types
import numpy as np
import ml_dtypes
import concourse.bass as bass
import concourse.mybir as mybir
from concourse.bass_utils import run_bass_kernel_spmd

F32 = mybir.dt.float32
BF16 = mybir.dt.bfloat16
I32 = mybir.dt.int32
AF = mybir.ActivationFunctionType
ALU = mybir.AluOpType

S = 8192
D = 1024
NOWN = 2048
DFF = 4096
MASKV = -3.0e5
RESCHED_SEGS = (0, 1, 3, 4)
EPS = 1e-6
PI = float(np.pi)

C_CQ, C_CKV, C_KR, C_FQ, C_FK, C_FV, C_F, C_GA, C_GB = 0, 256, 384, 416, 928, 1440, 1952, 1960, 2984


class Tok:
    __slots__ = ("writers", "readers", "gen_deps")

    def __init__(self):
        self.writers = []
        self.readers = []
        self.gen_deps = []


def _snap(f):
    if f.__closure__ is None:
        return f
    cells = []
    for c in f.__closure__:
        try:
            cells.append(types.CellType(c.cell_contents))
        except ValueError:
            cells.append(c)
    return types.FunctionType(f.__code__, f.__globals__, f.__name__, f.__defaults__, tuple(cells))


_DEF_N = {"pe": 512, "act": 512, "dve": 512, "pool": 1024, "sp": 512}


def _cost(eng, dma, n):
    if dma:
        return 600.0 if eng == "pool" else 60.0
    if n is None:
        n = _DEF_N[eng]
    if eng == "pe":
        return max(n, 64) / 2.4 + 4.0
    if eng == "act":
        return n / 1.2 + 170.0
    if eng == "dve":
        return n / 0.96 + 100.0
    if eng == "pool":
        return n / 0.9 + 150.0
    return 50.0


class Ins:
    __slots__ = ("eng", "fn", "idx", "deps", "sig", "signo", "dma", "dsem", "dval", "dprev", "cost", "seg", "bar", "nosig")

    def __init__(self, eng, fn, dma):
        self.eng = eng
        self.fn = fn
        self.dma = dma
        self.cost = 0.0
        self.seg = 0
        self.bar = False
        self.nosig = False
        self.deps = []
        self.sig = False
        self.signo = 0
        self.dsem = None
        self.dval = 0
        self.dprev = None


class Prog:
    ENGS = ("pe", "act", "dve", "pool", "sp")
    NRING = 12

    def __init__(self):
        self.ins = []
        self.last = {e: None for e in self.ENGS}
        self.dmas_since_barrier = []
        self.seg = 0

    def _add(self, eng, fn, reads, writes, dma, cow=(), track=True, n=None):
        i = Ins(eng, _snap(fn), dma)
        i.idx = len(self.ins)
        i.cost = _cost(eng, dma, n)
        i.seg = self.seg
        i.nosig = not track
        deps = {}
        for t in reads:
            for w in t.writers:
                deps[w.idx] = w
        for t in writes:
            g = list(t.writers) + list(t.readers)
            for w in g:
                deps[w.idx] = w
            t.gen_deps = g
        for t in cow:
            for w in t.gen_deps:
                deps[w.idx] = w
        for t in reads:
            t.readers.append(i)
        for t in writes:
            t.writers = [i]
            t.readers = []
        for t in cow:
            t.writers.append(i)
        deps.pop(i.idx, None)
        i.deps = list(deps.values())
        self.ins.append(i)
        if dma:
            self.dmas_since_barrier.append(i)
        elif track:
            self.last[eng] = i
        return i

    def op(self, eng, fn, reads=(), writes=(), cow=(), track=True, n=None):
        return self._add(eng, fn, reads, writes, False, cow, track, n)

    def dma(self, q, fn, reads=(), writes=(), cow=()):
        return self._add(q, fn, reads, writes, True, cow)

    def reschedule(self, segs):
        by_seg = {}
        for i in self.ins:
            by_seg.setdefault(i.seg, []).append(i)
        new = []
        for sg in sorted(by_seg):
            lst = by_seg[sg]
            if sg in segs:
                lst = self._list_schedule(lst)
            new.extend(lst)
        self.ins = new
        for k, i in enumerate(new):
            i.idx = k

    def _list_schedule(self, lst):
        LAT = 150.0
        DMA_LAT = 2500.0
        bars = [i for i in lst if i.bar]
        ops = [i for i in lst if not i.bar]
        inseg = {id(i) for i in ops}
        users = {id(i): [] for i in ops}
        indeg = {}
        for i in ops:
            k = 0
            for d in i.deps:
                if id(d) in inseg:
                    users[id(d)].append(i)
                    k += 1
            indeg[id(i)] = k
        start, fin = {}, {}
        depready = {}
        ready = {e: [] for e in self.ENGS}
        for i in ops:
            if indeg[id(i)] == 0:
                depready[id(i)] = 0.0
                ready[i.eng].append(i)
        eng_free = {e: 0.0 for e in self.ENGS}
        dma_hist = {e: [] for e in self.ENGS}
        order = []
        remaining = len(ops)
        while remaining:
            best = None
            for e in self.ENGS:
                rl = ready[e]
                if not rl:
                    continue
                ef = eng_free[e]
                cand = None
                for i in rl:
                    st = depready[id(i)]
                    if i.dma:
                        h = dma_hist[e]
                        if len(h) >= self.NRING:
                            st = max(st, h[len(h) - self.NRING])
                    st = max(st, ef)
                    key = (st, i.idx)
                    if cand is None or key < cand[0]:
                        cand = (key, i)
                if best is None or cand[0] < best[0]:
                    best = cand
            (st, _), i = best
            ready[i.eng].remove(i)
            start[id(i)] = st
            if i.dma:
                eng_free[i.eng] = st + i.cost
                f = st + i.cost + DMA_LAT
                dma_hist[i.eng].append(f)
            else:
                f = st + i.cost
                eng_free[i.eng] = f
            fin[id(i)] = f
            order.append(i)
            remaining -= 1
            for u in users[id(i)]:
                indeg[id(u)] -= 1
                if indeg[id(u)] == 0:
                    dr = 0.0
                    for d in u.deps:
                        if id(d) in inseg:
                            if d.eng == u.eng == "pe" and not u.dma:
                                t = fin[id(d)]
                            else:
                                t = fin[id(d)] + LAT
                            if t > dr:
                                dr = t
                    depready[id(u)] = dr
                    ready[u.eng].append(u)
        self.sim_end = max(fin.values()) if fin else 0.0
        lasts = {}
        for i in order:
            if not i.dma and not i.nosig:
                lasts[i.eng] = i
        dmas = [i for i in order if i.dma]
        for b_ in bars:
            b_.deps = [v for e, v in lasts.items() if e != b_.eng] + dmas
        return order + bars

    def mark(self, tok):
        tok.writers.append(self.ins[-1])

    def barrier(self):
        lasts = [v for v in self.last.values() if v is not None]
        dmas = list(self.dmas_since_barrier)
        self.dmas_since_barrier = []
        for e in self.ENGS:
            i = Ins(e, lambda eng: None, False)
            i.idx = len(self.ins)
            i.deps = [d for d in lasts if d.eng != e] + dmas
            i.seg = self.seg
            i.bar = True
            self.ins.append(i)
        self.seg += 1

    def emit(self, sems):
        per_eng = {e: [] for e in self.ENGS}
        for i in self.ins:
            per_eng[i.eng].append(i)
        for i in self.ins:
            for d in i.deps:
                if d.dma:
                    continue
                if d.eng == i.eng and not i.dma and d.eng == "pe":
                    continue
                d.sig = True
        cnt = {e: 0 for e in self.ENGS}
        dcount = {e: 0 for e in self.ENGS}
        lastring = {e: {} for e in self.ENGS}
        for i in self.ins:
            if i.dma:
                k = dcount[i.eng]
                dcount[i.eng] += 1
                slot = k % self.NRING
                i.dsem = sems["ring"][i.eng][slot]
                i.dval = 16 * (k // self.NRING + 1)
                i.dprev = lastring[i.eng].get(slot)
                lastring[i.eng][slot] = i
            elif i.sig:
                cnt[i.eng] += 1
                i.signo = cnt[i.eng]

        def run_engine(engname, eng):
            waited = {}
            for i in per_eng[engname]:
                need = {}
                deps = list(i.deps)
                if i.dma and i.dprev is not None:
                    deps.append(i.dprev)
                for d in deps:
                    if d.dma:
                        s, v = d.dsem, d.dval
                    else:
                        if d.eng == engname and not i.dma and engname == "pe":
                            continue
                        s, v = sems["cnt"][d.eng], d.signo
                    key = id(s)
                    if need.get(key, (None, 0))[1] < v:
                        need[key] = (s, v)
                for key, (s, v) in need.items():
                    if waited.get(key, 0) >= v:
                        continue
                    eng.wait_ge(s, v)
                    waited[key] = v
                r = i.fn(eng)
                if r is None:
                    continue
                if i.dma:
                    r.then_inc(i.dsem, 16)
                elif i.sig:
                    r.then_inc(sems["cnt"][engname], 1)

        return run_engine


class Ring:
    def __init__(self, n):
        self.n = n
        self.toks = [Tok() for _ in range(n)]
        self.i = -1

    def next(self):
        self.i += 1
        k = self.i % self.n
        return k, self.toks[k]


def build_program(dbg=False):
    nc = bass.Bass("TRN2", target_bir_lowering=False)
    P = Prog()

    def din(name, shape, dt=F32):
        return nc.dram_tensor(name, list(shape), dt, kind="ExternalInput")

    def dscr(name, shape, dt):
        return nc.dram_tensor(name, list(shape), dt, kind=("ExternalOutput" if dbg else "Internal"))

    xb_h = din("xb", [S, D])
    xo_h = din("xo", [NOWN, D])
    posb_h = din("posb", [128, 64], I32)
    poso_h = din("poso", [128, 16], I32)
    sel_h = din("sel", [8, 4])
    mask_h = din("maskT", [128, 16 * 128], BF16)
    ropec_h = din("ropec", [128, 64])
    w_in_h = din("w_in", [D, 4008])
    b_in_h = din("b_in", [4008])
    g_pre_mix_h = din("ln_pre_mix", [D])
    g_post_mix_h = din("ln_post_mix", [D])
    g_pre_mlp_h = din("ln_pre_mlp", [D])
    g_post_mlp_h = din("ln_post_mlp", [D])
    q_a_norm_h = din("q_a_norm", [256])
    kv_a_norm_h = din("kv_a_norm", [128])
    w_uq_h = din("w_uq", [256, 768])
    w_uk_h = din("w_uk", [128, 512])
    w_uv_h = din("w_uv", [128, 512])
    w_om_h = din("w_o_mla", [512, D])
    w_of_h = din("w_o_fox", [512, D])
    w_out_h = din("w_out", [D, D])
    w_ff1_h = din("w_ff1", [D, DFF])
    w_ff2_h = din("w_ff2", [DFF, D])
    out_h = nc.dram_tensor("out", [NOWN, D], F32, kind="ExternalOutput")

    KTf = dscr("KTf", [8, 70, S], BF16).ap()
    Vf = dscr("Vf", [8, 128, 64 * 65], BF16).ap()
    KTm = dscr("KTm", [8, 96, S], BF16).ap()
    Vm = dscr("Vm", [8, 128, 64 * 65], BF16).ap()
    QTf = dscr("QTf", [8, 70, NOWN], BF16).ap()
    QTm = dscr("QTm", [8, 96, NOWN], BF16).ap()
    SG = dscr("SG", [2, 8, 128, NOWN], BF16).ap()
    OT = dscr("OT", [2, 512, NOWN], BF16).ap()
    X1 = dscr("X1", [NOWN, D], F32).ap()
    H2T = dscr("H2T", [4, 128, 8 * 512], BF16).ap()
    WGb = nc.dram_tensor("WGb", [D, 2048], BF16).ap()
    FF2b = nc.dram_tensor("FF2b", [DFF, D], BF16).ap()
    t_WGb, t_FF2b = Tok(), Tok()
    t_KTf, t_Vf, t_KTm, t_Vm, t_QTf, t_QTm, t_SG, t_OT, t_X1, t_H2T, t_out = [Tok() for _ in range(11)]

    xb, xo = xb_h.ap(), xo_h.ap()
    w_in, b_in = w_in_h.ap(), b_in_h.ap()
    outp = out_h.ap()

    def bcast_rows(h, off, n):
        return bass.AP(h, off, [[0, 128], [1, n]])

    def colvec(h, off, nchunks, p=128):
        return bass.AP(h, off, [[1, p], [p, nchunks]])

    def mk_alloc(es):
        def sb(name, shape, dt):
            return es.enter_context(nc.sbuf_tensor(name, list(shape), dt))
        return sb

    with contextlib.ExitStack() as es0:
        sb0 = mk_alloc(es0)
        sems = {"cnt": {}, "ring": {}}
        for e in Prog.ENGS:
            sems["cnt"][e] = es0.enter_context(nc.semaphore("c_" + e))
            sems["ring"][e] = [es0.enter_context(nc.semaphore(f"r_{e}{k}")) for k in range(Prog.NRING)]

        ident = sb0("ident", [128, 128], BF16)
        identf = sb0("identf", [128, 128], F32)
        onesf = sb0("onesf", [128, 64], F32)
        cst = sb0("cst", [128, 4], F32)
        stat = sb0("stat", [128, 64], F32)
        junk = sb0("junk", [128, D], BF16)
        t_junk = [Tok(), Tok()]
        t_ident, t_identf, t_onesf, t_cst = [Tok() for _ in range(4)]
        gain_hs = [g_pre_mix_h, g_post_mix_h, g_pre_mlp_h, g_post_mlp_h]

        def load_gains(sbx, name, idxs):
            gt = sbx(name, [128, len(idxs), D], F32)
            tk = Tok()
            for k, gi in enumerate(idxs):
                P.dma("sp", lambda e: e.dma_start(out=gt[:, k, :], in_=bcast_rows(gain_hs[gi], 0, D)), [], [], [tk])
            return gt, tk
        stat_ring = Ring(16)

        P.op("pool", lambda e: e.memset(identf[:], 0.0), [], [t_identf])
        P.op("pool", lambda e: e.affine_select(out=identf[:], in_=identf[:], pattern=[[-1, 128]], compare_op=ALU.not_equal,
                                              fill=1.0, base=0, channel_multiplier=1), [t_identf], [t_identf])
        P.op("dve", lambda e: e.tensor_copy(out=ident[:], in_=identf[:]), [t_identf], [t_ident])
        P.op("dve", lambda e: e.memset(onesf[:], 1.0), [], [t_onesf])
        P.op("dve", lambda e: e.memset(cst[:, 0:1], EPS), [], [t_cst])
        P.op("dve", lambda e: e.memset(cst[:, 1:2], 1.0), [t_cst], [t_cst])
        P.op("dve", lambda e: e.memset(cst[:, 2:3], 0.0), [t_cst], [t_cst])
        P.op("dve", lambda e: e.memset(cst[:, 3:4], -1.0), [t_cst], [t_cst])

        def rstd_from_ss(ss_ap, n_feat, reads, rs_out, tok_out):
            P.op("act", lambda e: e.activation(out=rs_out[:, 1:2], in_=ss_ap, func=AF.Ln, scale=1.0 / n_feat, bias=cst[:, 0:1]),
                 reads + [t_cst], [tok_out], n=1)
            P.op("act", lambda e: e.activation(out=rs_out[:, 0:1], in_=rs_out[:, 1:2], func=AF.Exp, scale=-0.5), [tok_out], [tok_out], n=1)

        evac_flip = [0]

        def evac_copy(out_ap, in_ap, reads, writes, eng=None, cow=(), n=512):
            if eng is None:
                evac_flip[0] ^= 1
                eng = "act" if evac_flip[0] else "dve"
            if eng == "act":
                P.op("act", lambda e: e.activation(out=out_ap, in_=in_ap, func=AF.Copy), reads, writes, cow, n=n)
            else:
                P.op("dve", lambda e: e.tensor_copy(out=out_ap, in_=in_ap), reads, writes, cow, n=n)

        def new_stat():
            k, st_tok = stat_ring.next()
            return stat[:, 4 * k:4 * k + 4], st_tok

        with contextlib.ExitStack() as esP12:
            pa = [esP12.enter_context(nc.psum_tensor(f"pa{k}", [128, 512], F32)) for k in range(6)]
            ptb = [esP12.enter_context(nc.psum_tensor(f"pt{k}", [128, 1024], BF16)) for k in range(2)]
            pa_ring = Ring(6)
            pt_ring = Ring(2)
            esA = esP12
            sbA = mk_alloc(esA)
            C_all = sbA("C_all", [8, S], F32)
            t_Call = [Tok() for _ in range(16)]
            XS = 4
            xt = sbA("xt", [128, XS, D], F32)
            xt_ring = Ring(XS)
            xn = sbA("xn", [128, 3, D], BF16)
            xn_ring = Ring(3)
            hT = sbA("hT", [128, 2, 8, 512], BF16)
            hT_toks = [[Tok() for _ in range(4)] for _ in range(2)]
            gbcA, t_gbcA = load_gains(sbA, "gbcA", [0])

            def normA(x_ap, x_tok, g_ap, g_tok, xn_t, ring):
                st, st_tok = new_stat()
                xk, xtok = ring.next()
                P.op("act", lambda e: e.activation(out=xn_t[:, xk, :], in_=x_ap, func=AF.Square, accum_out=st[:, 0:1]), [x_tok], [st_tok, xtok], n=1024)
                rstd_from_ss(st[:, 0:1], D, [st_tok], st[:, 2:4], st_tok)
                P.op("dve", lambda e: e.scalar_tensor_tensor(out=xn_t[:, xk, :], in0=x_ap, scalar=st[:, 2:3], in1=g_ap,
                                                            op0=ALU.mult, op1=ALU.mult), [x_tok, st_tok, g_tok], [xtok], n=1024)
                return xk, xtok

            def normB(xn_t, xk, xtok, dst_ap, dst_tok, pt_list, ptring):
                pk, ptok = ptring.next()
                for fc in range(8):
                    P.op("pe", lambda e: e.transpose(out=pt_list[pk][:, fc * 128:(fc + 1) * 128], in_=xn_t[:, xk, fc * 128:(fc + 1) * 128],
                                                    identity=ident[:]), [xtok, t_ident], [ptok], n=128)
                evac_copy(dst_ap, pt_list[pk][:].rearrange("p (c t) -> p c t", c=8), [ptok], [dst_tok], n=1024)

            rowt = sbA("rowt", [16, 128], F32)
            t_rowt = Tok()

            def load_colvec(dst_ap, h, off, C, reads_tok_cow):
                P.dma("sp", lambda e: e.dma_start(out=rowt[0:C, :], in_=bass.AP(h, off, [[128, C], [1, 128]])), [], [t_rowt])
                pk, ptok = pa_ring.next()
                P.op("pe", lambda e: e.matmul(pa[pk][:, 0:C], lhsT=rowt[0:C, :], rhs=identf[0:C, 0:C], start=True, stop=True), [t_rowt, t_identf], [ptok], n=64)
                P.op("dve", lambda e: e.tensor_copy(out=dst_ap, in_=pa[pk][:, 0:C]), [ptok], [], [reads_tok_cow], n=16)

            tilesA, normed = {}, {}

            def sA_load(ci):
                src, r0 = (xb, ci * 512) if ci < 16 else (xo, (ci - 16) * 512)
                lst = []
                for t in range(4):
                    k, tk = xt_ring.next()
                    P.dma("sp", lambda e: e.dma_start(out=xt[:, k, :], in_=src[r0 + t * 128:r0 + (t + 1) * 128, :]), [], [tk])
                    lst.append((k, tk))
                tilesA[ci] = lst

            def sA_normA(ci, t):
                k, tk = tilesA[ci][t]
                normed[(ci, t)] = normA(xt[:, k, :], tk, gbcA[:, 0, :], t_gbcA, xn, xn_ring)

            def sA_normB(ci, t):
                xk, xtok = normed[(ci, t)]
                normB(xn, xk, xtok, hT[:, ci % 2, :, t * 128:(t + 1) * 128], hT_toks[ci % 2][t], ptb, pt_ring)

            def rope_table(sbx, name, pos_h, NT):
                posi = sbx(name + "_pi", [128, NT], I32)
                posf = sbx(name + "_pf", [128, NT], F32)
                rc = sbx(name + "_rc", [128, 64], F32)
                ang = sbx(name + "_ang", [128, NT, 32], F32)
                tq = sbx(name + "_tq", [128, NT, 32], F32)
                ki = sbx(name + "_ki", [128, NT, 32], I32)
                tk = Tok()
                tcs = Tok()
                P.dma("sp", lambda e: e.dma_start(out=posi[:], in_=pos_h.ap()), [], [tk])
                P.dma("sp", lambda e: e.dma_start(out=rc[:], in_=ropec_h.ap()), [], [], [tk])
                P.op("dve", lambda e: e.tensor_copy(out=posf[:], in_=posi[:]), [tk], [tk])
                posb3 = bass.AP(posf, 0, [[NT, 128], [1, NT], [0, 32]])
                invb3 = bass.AP(rc, 0, [[64, 128], [0, NT], [1, 32]])
                shb3 = bass.AP(rc, 32, [[64, 128], [0, NT], [1, 32]])
                P.op("dve", lambda e: e.tensor_tensor(out=ang[:], in0=posb3, in1=invb3, op=ALU.mult), [tk], [tk])
                P.op("dve", lambda e: e.tensor_tensor(out=ang[:], in0=ang[:], in1=shb3, op=ALU.add), [tk], [tk])
                P.op("dve", lambda e: e.tensor_scalar(out=tq[:], in0=ang[:], scalar1=float(1.0 / (2 * PI)), scalar2=None, op0=ALU.mult), [tk], [tk])
                P.op("dve", lambda e: e.tensor_copy(out=ki[:], in_=tq[:]), [tk], [tk])
                P.op("dve", lambda e: e.tensor_copy(out=tq[:], in_=ki[:]), [tk], [tk])
                P.op("dve", lambda e: e.scalar_tensor_tensor(out=ang[:], in0=tq[:], scalar=float(-2 * PI), in1=ang[:], op0=ALU.mult, op1=ALU.add), [tk], [tk])
                P.op("dve", lambda e: e.tensor_scalar(out=ang[:], in0=ang[:], scalar1=-PI, scalar2=PI, op0=ALU.max, op1=ALU.min), [tk], [tk])
                P.op("act", lambda e: e.activation(out=tq[:], in_=ang[:], func=AF.Sin), [tk], [tcs])
                return tq, tcs

            wQ_s = sbA("wQ", [128, 8, 768], BF16)
            wuq_s = sbA("wuq", [128, 2, 768], BF16)
            t_wQ_s, t_wuq_s = Tok(), Tok()
            sA_load(0)
            esA1 = contextlib.ExitStack()
            sb1 = mk_alloc(esA1)
            wK = sb1("wK", [128, 8, 1192], BF16)
            wuk = sb1("wuk", [128, 512], BF16)
            wuv = sb1("wuv", [128, 512], BF16)
            t_wK, t_wuk, t_wuv = Tok(), Tok(), Tok()
            for fc in range(8):
                P.dma("pool", lambda e: e.dma_start(out=wK[:, fc, 0:1032], in_=w_in[fc * 128:(fc + 1) * 128, C_FK:C_FK + 1032]), [], [], [t_wK])
                P.dma("pool", lambda e: e.dma_start(out=wK[:, fc, 1032:1192], in_=w_in[fc * 128:(fc + 1) * 128, C_CKV:C_CKV + 160]), [], [], [t_wK])
            P.dma("pool", lambda e: e.dma_start(out=wuk[:], in_=w_uk_h.ap()), [], [], [t_wuk])
            P.dma("pool", lambda e: e.dma_start(out=wuv[:], in_=w_uv_h.ap()), [], [], [t_wuv])
            t_mark1 = Tok()
            p2_pref = []
            for fc in range(8):
                p2_pref.append((lambda e, fc=fc: e.dma_start(out=wQ_s[:, fc, 0:512], in_=w_in[fc * 128:(fc + 1) * 128, C_FQ:C_FQ + 512]), t_wQ_s))
                p2_pref.append((lambda e, fc=fc: e.dma_start(out=wQ_s[:, fc, 512:768], in_=w_in[fc * 128:(fc + 1) * 128, C_CQ:C_CQ + 256]), t_wQ_s))
            for lc in range(2):
                p2_pref.append((lambda e, lc=lc: e.dma_start(out=wuq_s[:, lc, :], in_=w_uq_h.ap()[lc * 128:(lc + 1) * 128, :]), t_wuq_s))
            for fc in range(8):
                p2_pref.append((lambda e, fc=fc: e.dma_start(out=WGb[fc * 128:(fc + 1) * 128, :], in_=w_in[fc * 128:(fc + 1) * 128, C_GA:C_GA + 2048]), t_WGb))

            def emit_p2_prefetch(k):
                mk = Tok()
                P.mark(mk)
                for _ in range(k):
                    if p2_pref:
                        fn_, tk_ = p2_pref.pop(0)
                        P.dma("pool", fn_, [mk], [], [tk_])

            b_fk = sb1("b_fk", [128, 4], F32)
            nb_f = sb1("nb_f", [8, 2], F32)
            b_fv = sb1("b_fv", [128, 512], F32)
            b_ckr = sb1("b_ckr", [128, 160], F32)
            g_kv = sb1("g_kv", [128, 1], F32)
            t_bias1 = Tok()
            load_colvec(b_fk[:], b_in_h, C_FK, 4, t_bias1)
            P.dma("sp", lambda e: e.dma_start(out=nb_f[:, 0:1], in_=colvec(b_in_h, C_F, 1, p=8), allow_slow_non_contiguous=True), [], [], [t_bias1])
            P.dma("sp", lambda e: e.dma_start(out=b_fv[:], in_=bcast_rows(b_in_h, C_FV, 512)), [], [], [t_bias1])
            P.dma("sp", lambda e: e.dma_start(out=b_ckr[:], in_=bcast_rows(b_in_h, C_CKV, 160)), [], [], [t_bias1])
            P.dma("sp", lambda e: e.dma_start(out=g_kv[:], in_=colvec(kv_a_norm_h, 0, 1), allow_slow_non_contiguous=True), [], [], [t_bias1])
            t_nbf = Tok()
            P.op("dve", lambda e: e.tensor_scalar(out=nb_f[:, 1:2], in0=nb_f[:, 0:1], scalar1=-1.0, scalar2=None, op0=ALU.mult), [t_bias1], [t_nbf])
            rope_hold = {}

            kst = sb1("kst", [128, 4, 512], BF16)
            kst_ring = Ring(4)
            vst = sb1("vst", [128, 2, 8, 260], BF16)
            vst_toks = [Tok(), Tok()]
            vmst = sb1("vmst", [128, 2, 8, 260], BF16)
            vmst_toks = [Tok(), Tok()]
            for s_ in range(2):
                P.op("pool", lambda e: e.memset(vst[:, s_], 1.0), [], [vst_toks[s_]])
                P.op("pool", lambda e: e.memset(vmst[:, s_], 1.0), [], [vmst_toks[s_]])
            ckr = sb1("ckr", [128, 2, 4, 160], F32)
            ckr_toks = [[Tok() for _ in range(4)] for _ in range(2)]
            ckn = sb1("ckn", [128, 2, 4, 128], BF16)
            ckn_toks = [[Tok() for _ in range(4)] for _ in range(2)]
            ckvT = sb1("ckvT", [128, 2, 512], BF16)
            ckvT_toks = [[Tok() for _ in range(4)] for _ in range(2)]
            rtmp = sb1("rtmp", [128, 4, 4, 16], F32)
            t_rtmp = Tok()
            krot = sb1("krot", [128, 2, 4, 32], BF16)
            krot_toks = [Tok(), Tok()]
            krT = sb1("krT", [32, 2, 512], BF16)
            krT_toks = [Tok(), Tok()]
            knst = sb1("knst", [128, 4, 512], BF16)
            knst_ring = Ring(4)
            fg = sb1("fg", [8, 2, 512], F32)
            t_fg = Tok()
            fones = sb1("fones", [8, 512], F32)
            t_fones = Tok()
            P.op("pool", lambda e: e.memset(fones[:], 1.0), [], [t_fones])
            fw_ = sb1("fw", [8, 2, 2, 512], F32)
            t_fw = [Tok(), Tok()]
            f6 = sb1("f6", [8, 2, 6, 512], BF16)
            f6_toks = [Tok(), Tok()]
            for s_ in range(2):
                P.op("pool", lambda e: e.memset(f6[:, s_], 1.0), [], [f6_toks[s_]])

            def split3(src_ap, dst3, reads, dst_tok, mul, wt, wtok):
                r1, r2 = wt[:, 0, :], wt[:, 1, :]

                def st_a():
                    P.op("dve", lambda e: e.tensor_scalar(out=dst3[:, 0, :], in0=src_ap, scalar1=mul, scalar2=None, op0=ALU.mult), reads, [dst_tok])
                    P.op("dve", lambda e: e.scalar_tensor_tensor(out=r1, in0=src_ap, scalar=mul, in1=dst3[:, 0, :], op0=ALU.mult, op1=ALU.subtract),
                         reads + [dst_tok], [wtok])

                def st_b():
                    P.op("dve", lambda e: e.tensor_copy(out=dst3[:, 1, :], in_=r1), [wtok], [], [dst_tok])
                    P.op("dve", lambda e: e.tensor_tensor(out=r2, in0=r1, in1=dst3[:, 1, :], op=ALU.subtract), [wtok, dst_tok], [], [wtok])

                def st_c():
                    P.op("dve", lambda e: e.tensor_copy(out=dst3[:, 2, :], in_=r2), [wtok], [], [dst_tok])
                return [st_a, st_b, st_c]

            pend_split = []

            def drain_split(n=1):
                for _ in range(n):
                    if pend_split:
                        pend_split.pop(0)()

            def p1_slots(c):
                hs = c % 2
                cs_ = c % 2
                vs = c % 2
                cols = slice(c * 512, (c + 1) * 512)
                hTr = hT_toks[hs]

                def kproj(m):
                    pk, ptok = pa_ring.next()
                    for fc in range(8):
                        P.op("pe", lambda e: e.matmul(pa[pk][:], lhsT=wK[:, fc, m * 128:(m + 1) * 128], rhs=hT[:, hs, fc, :],
                                                      start=(fc == 0), stop=(fc == 7)), hTr + [t_wK], [ptok])
                    sk, stok = kst_ring.next()
                    P.op("act", lambda e: e.activation(out=kst[:, sk, :], in_=pa[pk][:], func=AF.Identity, bias=b_fk[:, m:m + 1]),
                         [ptok, t_bias1], [stok])
                    for hh in range(2):
                        P.dma("sp", lambda e: e.dma_start(out=KTf[2 * m + hh, 0:64, cols], in_=kst[hh * 64:(hh + 1) * 64, sk, :]), [stok], [], [t_KTf])

                def vproj(t):
                    pk, ptok = pa_ring.next()
                    for fc in range(8):
                        P.op("pe", lambda e: e.matmul(pa[pk][:], lhsT=hT[:, hs, fc, t * 128:(t + 1) * 128], rhs=wK[:, fc, 512:1024],
                                                      start=(fc == 0), stop=(fc == 7)), [hTr[t], t_wK], [ptok])
                    dst = bass.AP(vst, vs * 8 * 260 + t * 65, [[2 * 8 * 260, 128], [260, 8], [1, 64]])
                    P.op("dve", lambda e: e.tensor_tensor(out=dst, in0=pa[pk][:].rearrange("p (h d) -> p h d", h=8),
                                                         in1=b_fv[:].rearrange("p (h d) -> p h d", h=8), op=ALU.add),
                         [ptok, t_bias1], ([vst_toks[vs]] if t == 0 else []), ([] if t == 0 else [vst_toks[vs]]))
                    if t == 3:
                        P.dma("sp", lambda e: e.dma_start(out=Vf[:, :, c * 260:(c + 1) * 260].rearrange("h p x -> p h x"), in_=vst[:, vs]), [vst_toks[vs]], [], [t_Vf])

                def ckr_mm(t):
                    pk, ptok = pa_ring.next()
                    for fc in range(8):
                        P.op("pe", lambda e: e.matmul(pa[pk][:, 0:160], lhsT=hT[:, hs, fc, t * 128:(t + 1) * 128], rhs=wK[:, fc, 1032:1192],
                                                      start=(fc == 0), stop=(fc == 7)), [hTr[t], t_wK], [ptok], n=160)
                    P.op("dve", lambda e: e.tensor_tensor(out=ckr[:, cs_, t, :], in0=pa[pk][:, 0:160], in1=b_ckr[:], op=ALU.add),
                         [ptok, t_bias1], [ckr_toks[cs_][t]], n=160)
                    st, st_tok = new_stat()
                    P.op("act", lambda e: e.activation(out=ckn[:, cs_, t, :], in_=ckr[:, cs_, t, 0:128], func=AF.Square, accum_out=st[:, 0:1]),
                         [ckr_toks[cs_][t]], [st_tok, ckn_toks[cs_][t]], n=128)
                    rstd_from_ss(st[:, 0:1], 128, [st_tok], st[:, 2:4], st_tok)
                    P.op("dve", lambda e: e.tensor_scalar(out=ckn[:, cs_, t, :], in0=ckr[:, cs_, t, 0:128], scalar1=st[:, 2:3], scalar2=None, op0=ALU.mult),
                         [ckr_toks[cs_][t], st_tok], [ckn_toks[cs_][t]], n=128)

                def ckn_T():
                    tpk, tptok = pt_ring.next()
                    for t in range(4):
                        P.op("pe", lambda e: e.transpose(out=ptb[tpk][:, t * 128:(t + 1) * 128], in_=ckn[:, cs_, t, :], identity=ident[:]),
                             [ckn_toks[cs_][t], t_ident], [tptok], n=128)
                    P.op("dve", lambda e: e.tensor_scalar(out=ckvT[:, cs_, :], in0=ptb[tpk][:, 0:512], scalar1=g_kv[:, 0:1], scalar2=None, op0=ALU.mult),
                         [tptok, t_bias1], [ckvT_toks[cs_][0]])

                def rope_k():
                    x1 = ckr[:, cs_, :, 128:144]
                    x2 = ckr[:, cs_, :, 144:160]
                    csb, t_csb = rope_hold["cs"], rope_hold["tok"]
                    sn = csb[:, 4 * c:4 * c + 4, 0:16]
                    co = csb[:, 4 * c:4 * c + 4, 16:32]
                    rr = ckr_toks[cs_] + [t_csb]
                    P.op("dve", lambda e: e.tensor_tensor(out=rtmp[:, 0], in0=x1, in1=co, op=ALU.mult), rr, [t_rtmp], n=64)
                    P.op("dve", lambda e: e.tensor_tensor(out=rtmp[:, 1], in0=x2, in1=sn, op=ALU.mult), rr, [], [t_rtmp], n=64)
                    P.op("dve", lambda e: e.tensor_tensor(out=rtmp[:, 2], in0=x2, in1=co, op=ALU.mult), rr, [], [t_rtmp], n=64)
                    P.op("dve", lambda e: e.tensor_tensor(out=rtmp[:, 3], in0=x1, in1=sn, op=ALU.mult), rr, [], [t_rtmp], n=64)
                    P.op("dve", lambda e: e.tensor_tensor(out=krot[:, cs_, :, 0:16], in0=rtmp[:, 0], in1=rtmp[:, 1], op=ALU.subtract), [t_rtmp], [krot_toks[cs_]], n=64)
                    P.op("dve", lambda e: e.tensor_tensor(out=krot[:, cs_, :, 16:32], in0=rtmp[:, 2], in1=rtmp[:, 3], op=ALU.add), [t_rtmp], [], [krot_toks[cs_]], n=64)

                def krot_T():
                    tpk, tptok = pt_ring.next()
                    for t in range(4):
                        P.op("pe", lambda e: e.transpose(out=ptb[tpk][0:32, t * 128:(t + 1) * 128], in_=krot[:, cs_, t, :], identity=ident[:]),
                             [krot_toks[cs_], t_ident], [tptok], n=128)
                    P.op("act", lambda e: e.activation(out=krT[:, cs_, :], in_=ptb[tpk][0:32, 0:512], func=AF.Copy), [tptok], [krT_toks[cs_]])
                    for h in range(8):
                        P.dma("pool" if h % 2 else "sp", lambda e: e.dma_start(out=KTm[h, 64:96, cols], in_=krT[:, cs_, :]), [krT_toks[cs_]], [], [t_KTm])

                def fchain():
                    pk, ptok = pa_ring.next()
                    for fc in range(8):
                        P.op("pe", lambda e: e.matmul(pa[pk][0:8, :], lhsT=wK[:, fc, 1024:1032], rhs=hT[:, hs, fc, :], start=(fc == 0), stop=(fc == 7)),
                             hTr + [t_wK], [ptok])
                    P.op("act", lambda e: e.activation(out=fg[:, 0, :], in_=pa[pk][0:8, :], func=AF.Exp, scale=-1.0, bias=nb_f[:, 1:2]), [ptok, t_nbf], [t_fg])
                    P.op("act", lambda e: e.activation(out=fg[:, 1, :], in_=fg[:, 0, :], func=AF.Ln, bias=cst[0:8, 1:2]), [t_fg, t_cst], [t_fg])
                    init = 0.0 if c == 0 else C_all[:, c * 512 - 1:c * 512]
                    P.op("dve", lambda e: e.tensor_tensor_scan(out=C_all[:, cols], data0=fones[:], data1=fg[:, 1, :], initial=init, op0=ALU.mult, op1=ALU.add),
                         [t_fg, t_fones] + ([t_Call[c - 1]] if c else []), [t_Call[c]])
                    fs = c % 2
                    stg = split3(C_all[:, cols], f6[:, fs, 0:3, :], [t_Call[c]], f6_toks[fs], 8.0, fw_[:, fs], t_fw[fs])

                    def fin():
                        stg[2]()
                        P.dma("sp", lambda e: e.dma_start(out=KTf[:, 64:70, cols], in_=f6[:, fs]), [f6_toks[fs]], [], [t_KTf])
                    pend_split.extend([stg[0], stg[1], fin])

                def upproj():
                    for m in range(4):
                        pk, ptok = pa_ring.next()
                        P.op("pe", lambda e: e.matmul(pa[pk][:], lhsT=wuk[:, m * 128:(m + 1) * 128], rhs=ckvT[:, cs_, :], start=True, stop=True),
                             [ckvT_toks[cs_][0], t_wuk], [ptok])
                        sk, stok = knst_ring.next()
                        evac_copy(knst[:, sk, :], pa[pk][:], [ptok], [stok])
                        for hh in range(2):
                            P.dma("pool", lambda e: e.dma_start(out=KTm[2 * m + hh, 0:64, cols], in_=knst[hh * 64:(hh + 1) * 64, sk, :]), [stok], [], [t_KTm])
                    for t in range(4):
                        pk, ptok = pa_ring.next()
                        P.op("pe", lambda e: e.matmul(pa[pk][:], lhsT=ckvT[:, cs_, t * 128:(t + 1) * 128], rhs=wuv[:], start=True, stop=True),
                             [ckvT_toks[cs_][0], t_wuv], [ptok])
                        dst = bass.AP(vmst, vs * 8 * 260 + t * 65, [[2 * 8 * 260, 128], [260, 8], [1, 64]])
                        evac_copy(dst, pa[pk][:].rearrange("p (h d) -> p h d", h=8), [ptok], ([vmst_toks[vs]] if t == 0 else []), cow=([] if t == 0 else [vmst_toks[vs]]))
                    P.dma("pool", lambda e: e.dma_start(out=Vm[:, :, c * 260:(c + 1) * 260].rearrange("h p x -> p h x"), in_=vmst[:, vs]), [vmst_toks[vs]], [], [t_Vm])

                def s0():
                    for t in range(4):
                        ckr_mm(t)
                    kproj(0)
                    kproj(1)

                def s1():
                    kproj(2)
                    kproj(3)
                    ckn_T()
                    rope_k()
                    vproj(0)

                def s2():
                    vproj(1)
                    vproj(2)
                    vproj(3)

                def s3():
                    krot_T()
                    fchain()
                    upproj()
                return [s0, s1, s2, s3]

            p2 = {}

            def p2_setup():
                esA2 = contextlib.ExitStack()
                p2["es"] = esA2
                sb2 = mk_alloc(esA2)
                p2["wQ"] = wQ_s
                p2["wuq"] = wuq_s
                p2["wG"] = sb2("wG", [128, 8, 2048], BF16)
                p2["t_wQ"], p2["t_wuq"], p2["t_wG"] = t_wQ_s, t_wuq_s, Tok()
                wQ, wuq, wG = p2["wQ"], p2["wuq"], p2["wG"]
                for fc in range(8):
                    P.dma("sp", lambda e: e.dma_start(out=wG[:, fc, :], in_=WGb[fc * 128:(fc + 1) * 128, :]), [t_WGb], [], [p2["t_wG"]])
                p2["b_fq"] = sb2("b_fq", [128, 4], F32)
                p2["b_g"] = sb2("b_g", [128, 16], F32)
                p2["b_cq"] = sb2("b_cq", [128, 256], F32)
                p2["g_q"] = sb2("g_q", [128, 2], F32)
                p2["selt"] = sb2("selt", [8, 4], F32)
                p2["t_bias2"] = Tok()
                tb = p2["t_bias2"]
                load_colvec(p2["b_fq"][:], b_in_h, C_FQ, 4, tb)
                load_colvec(p2["b_g"][:], b_in_h, C_GA, 16, tb)
                P.dma("sp", lambda e: e.dma_start(out=p2["b_cq"][:], in_=bcast_rows(b_in_h, C_CQ, 256)), [], [], [tb])
                load_colvec(p2["g_q"][:], q_a_norm_h, 0, 2, tb)
                P.dma("sp", lambda e: e.dma_start(out=p2["selt"][:], in_=sel_h.ap()), [], [], [tb])
                p2["cso"], p2["t_cso"] = rope_table(sb2, "ro", poso_h, 16)
                p2["qst"] = sb2("qst", [128, 4, 512], BF16)
                p2["qst_ring"] = Ring(4)
                p2["cq"] = sb2("cq", [128, 2, 256], F32)
                p2["cq_ring"] = Ring(2)
                p2["cqn"] = sb2("cqn", [128, 4, 256], BF16)
                p2["cqn_toks"] = [Tok() for _ in range(4)]
                p2["cqT"] = sb2("cqT", [128, 2, 2, 512], BF16)
                p2["cqT_toks"] = [Tok(), Tok()]
                p2["qf"] = sb2("qf", [128, 2, 768], F32)
                p2["qf_ring"] = Ring(2)
                p2["qtmp"] = sb2("qtmp", [128, 4, 8, 16], F32)
                p2["t_qtmp"] = Tok()
                p2["qb"] = sb2("qb", [128, 4, 768], BF16)
                p2["qb_toks"] = [Tok() for _ in range(4)]
                p2["qmst"] = sb2("qmst", [96, 8, 512], BF16)
                p2["t_qmst"] = Tok()
                p2["sgst"] = sb2("sgst", [128, 4, 512], BF16)
                p2["sgst_ring"] = Ring(4)
                p2["co"] = sb2("co", [8, 512], F32)
                p2["t_co"] = Tok()
                p2["fw2"] = sb2("fw2", [8, 2, 512], F32)
                p2["t_fw2"] = Tok()
                p2["f6q"] = sb2("f6q", [8, 6, 512], BF16)
                p2["t_f6q"] = Tok()
                P.op("pool", lambda e: e.memset(p2["f6q"][:], 1.0), [], [p2["t_f6q"]])

            def p2_slots(j):
                hs = j % 2
                cols = slice(j * 512, (j + 1) * 512)
                hTr = hT_toks[hs]
                wQ, wuq, wG = p2["wQ"], p2["wuq"], p2["wG"]
                t_wQ, t_wuq, t_wG, tb = p2["t_wQ"], p2["t_wuq"], p2["t_wG"], p2["t_bias2"]
                b_fq, b_g, b_cq, g_q, selt = p2["b_fq"], p2["b_g"], p2["b_cq"], p2["g_q"], p2["selt"]
                cso, t_cso = p2["cso"], p2["t_cso"]
                qst, qst_ring, cq, cq_ring, cqn, cqn_toks = p2["qst"], p2["qst_ring"], p2["cq"], p2["cq_ring"], p2["cqn"], p2["cqn_toks"]
                cqs = j % 2
                cqT, cqT_tok = p2["cqT"], p2["cqT_toks"][cqs]
                qf, qf_ring, qtmp, t_qtmp, qb, qb_toks = p2["qf"], p2["qf_ring"], p2["qtmp"], p2["t_qtmp"], p2["qb"], p2["qb_toks"]
                qmst, t_qmst, sgst, sgst_ring = p2["qmst"], p2["t_qmst"], p2["sgst"], p2["sgst_ring"]
                co_, t_co, fw2, t_fw2, f6q, t_f6q = p2["co"], p2["t_co"], p2["fw2"], p2["t_fw2"], p2["f6q"], p2["t_f6q"]

                def qfox(m):
                    pk, ptok = pa_ring.next()
                    for fc in range(8):
                        P.op("pe", lambda e: e.matmul(pa[pk][:], lhsT=wQ[:, fc, m * 128:(m + 1) * 128], rhs=hT[:, hs, fc, :],
                                                      start=(fc == 0), stop=(fc == 7)), hTr + [t_wQ], [ptok])
                    sk, stok = qst_ring.next()
                    P.op("act", lambda e: e.activation(out=qst[:, sk, :], in_=pa[pk][:], func=AF.Identity, bias=b_fq[:, m:m + 1]), [ptok, tb], [stok])
                    for hh in range(2):
                        P.dma("sp", lambda e: e.dma_start(out=QTf[2 * m + hh, 0:64, cols], in_=qst[hh * 64:(hh + 1) * 64, sk, :]), [stok], [], [t_QTf])

                def ownF():
                    co3 = co_[:].rearrange("p (i r) -> p i r", i=4)
                    rC = [t_Call[4 * j + q] for q in range(4)]
                    for cc in range(4):
                        src = bass.AP(C_all, (16 * j + cc) * 128, [[S, 8], [4 * 128, 4], [1, 128]])
                        if cc == 0:
                            P.op("dve", lambda e: e.tensor_scalar(out=co3, in0=src, scalar1=selt[:, 0:1], scalar2=None, op0=ALU.mult),
                                 rC + [tb], [t_co])
                        else:
                            P.op("dve", lambda e: e.scalar_tensor_tensor(out=co3, in0=src, scalar=selt[:, cc:cc + 1], in1=co3,
                                                                        op0=ALU.mult, op1=ALU.add), rC + [tb, t_co], [t_co])
                    stg = split3(co_[:], f6q[:, 3:6, :], [t_co], t_f6q, -8.0, fw2, t_fw2)

                    def fin():
                        stg[2]()
                        P.dma("sp", lambda e: e.dma_start(out=QTf[:, 64:70, cols], in_=f6q[:]), [t_f6q], [], [t_QTf])
                    pend_split.extend([stg[0], stg[1], fin])

                def cq_mm(t):
                    pk, ptok = pa_ring.next()
                    for fc in range(8):
                        P.op("pe", lambda e: e.matmul(pa[pk][:, 0:256], lhsT=hT[:, hs, fc, t * 128:(t + 1) * 128], rhs=wQ[:, fc, 512:768],
                                                      start=(fc == 0), stop=(fc == 7)), [hTr[t], t_wQ], [ptok], n=256)
                    ck, ctok = cq_ring.next()
                    P.op("dve", lambda e: e.tensor_tensor(out=cq[:, ck, :], in0=pa[pk][:, 0:256], in1=b_cq[:], op=ALU.add), [ptok, tb], [ctok], n=256)
                    st, st_tok = new_stat()
                    P.op("act", lambda e: e.activation(out=cqn[:, t, :], in_=cq[:, ck, :], func=AF.Square, accum_out=st[:, 0:1]), [ctok], [st_tok, cqn_toks[t]], n=256)
                    rstd_from_ss(st[:, 0:1], 256, [st_tok], st[:, 2:4], st_tok)
                    P.op("dve", lambda e: e.tensor_scalar(out=cqn[:, t, :], in0=cq[:, ck, :], scalar1=st[:, 2:3], scalar2=None, op0=ALU.mult),
                         [ctok, st_tok], [cqn_toks[t]], n=256)

                def cq_T():
                    for half in range(2):
                        tpk, tptok = pt_ring.next()
                        for tt in range(2):
                            t = half * 2 + tt
                            for lc in range(2):
                                P.op("pe", lambda e: e.transpose(out=ptb[tpk][:, (tt * 2 + lc) * 128:(tt * 2 + lc + 1) * 128], in_=cqn[:, t, lc * 128:(lc + 1) * 128], identity=ident[:]),
                                     [cqn_toks[t], t_ident], [tptok], n=128)
                        for lc in range(2):
                            src = bass.AP(ptb[tpk], lc * 128, [[1024, 128], [256, 2], [1, 128]])
                            dst = bass.AP(cqT, cqs * 1024 + lc * 512 + half * 256, [[2048, 128], [128, 2], [1, 128]])
                            first = (half == 0 and lc == 0)
                            P.op("dve", lambda e: e.tensor_scalar(out=dst, in0=src, scalar1=g_q[:, lc:lc + 1], scalar2=None, op0=ALU.mult),
                                 [tptok, tb], ([cqT_tok] if first else []), ([] if first else [cqT_tok]), n=256)

                def qfull(t):
                    n = 4 * j + t
                    pk, ptok = pa_ring.next()
                    pk2, ptok2 = pa_ring.next()
                    for lc in range(2):
                        P.op("pe", lambda e: e.matmul(pa[pk][:], lhsT=cqT[:, cqs, lc, t * 128:(t + 1) * 128], rhs=wuq[:, lc, 0:512],
                                                      start=(lc == 0), stop=(lc == 1)), [cqT_tok, t_wuq], [ptok])
                    for lc in range(2):
                        P.op("pe", lambda e: e.matmul(pa[pk2][:, 0:256], lhsT=cqT[:, cqs, lc, t * 128:(t + 1) * 128], rhs=wuq[:, lc, 512:768],
                                                      start=(lc == 0), stop=(lc == 1)), [cqT_tok, t_wuq], [ptok2], n=256)
                    fk_, ftok = qf_ring.next()
                    evac_copy(qf[:, fk_, 0:512], pa[pk][:], [ptok], [ftok], eng="act")
                    evac_copy(qf[:, fk_, 512:768], pa[pk2][:, 0:256], [ptok2], [], eng="act", cow=[ftok], n=256)
                    btok = qb_toks[t]
                    q3 = qf[:, fk_, :].rearrange("p (h d) -> p h d", h=8)
                    qb3 = qb[:, t, :].rearrange("p (h d) -> p h d", h=8)
                    x1 = q3[:, :, 64:80]
                    x2 = q3[:, :, 80:96]
                    sn = bass.AP(cso, n * 32, [[16 * 32, 128], [0, 8], [1, 16]])
                    co = bass.AP(cso, n * 32 + 16, [[16 * 32, 128], [0, 8], [1, 16]])
                    rr = [ftok, t_cso]
                    P.op("dve", lambda e: e.tensor_tensor(out=qtmp[:, 0], in0=x1, in1=co, op=ALU.mult), rr, [t_qtmp], n=128)
                    P.op("dve", lambda e: e.tensor_tensor(out=qtmp[:, 1], in0=x2, in1=sn, op=ALU.mult), rr, [], [t_qtmp], n=128)
                    P.op("dve", lambda e: e.tensor_tensor(out=qtmp[:, 2], in0=x2, in1=co, op=ALU.mult), rr, [], [t_qtmp], n=128)
                    P.op("dve", lambda e: e.tensor_tensor(out=qtmp[:, 3], in0=x1, in1=sn, op=ALU.mult), rr, [], [t_qtmp], n=128)
                    P.op("dve", lambda e: e.tensor_tensor(out=qb3[:, :, 64:80], in0=qtmp[:, 0], in1=qtmp[:, 1], op=ALU.subtract), [t_qtmp], [btok], n=128)
                    P.op("dve", lambda e: e.tensor_tensor(out=qb3[:, :, 80:96], in0=qtmp[:, 2], in1=qtmp[:, 3], op=ALU.add), [t_qtmp], [], [btok], n=128)
                    P.op("act", lambda e: e.activation(out=qb3[:, :, 0:64], in_=q3[:, :, 0:64], func=AF.Copy), [ftok], [], [btok])

                def q_T():
                    for t in range(4):
                        tpk, tptok = pt_ring.next()
                        for h in range(8):
                            P.op("pe", lambda e: e.transpose(out=ptb[tpk][0:96, h * 128:(h + 1) * 128], in_=qb[:, t, h * 96:(h + 1) * 96], identity=ident[:]),
                                 [qb_toks[t], t_ident], [tptok], n=128)
                        evac_copy(qmst[:, :, t * 128:(t + 1) * 128], ptb[tpk][0:96, :].rearrange("p (h t) -> p h t", h=8), [tptok],
                                  ([t_qmst] if t == 0 else []), cow=([] if t == 0 else [t_qmst]), n=1024)
                    P.dma("sp", lambda e: e.dma_start(out=QTm[:, :, cols].rearrange("h r c -> r h c"), in_=qmst[:]), [t_qmst], [], [t_QTm])

                def gate(gi):
                    mixer, oc = gi // 8, gi % 8
                    pk, ptok = pa_ring.next()
                    for fc in range(8):
                        P.op("pe", lambda e: e.matmul(pa[pk][:], lhsT=wG[:, fc, gi * 128:(gi + 1) * 128], rhs=hT[:, hs, fc, :],
                                                      start=(fc == 0), stop=(fc == 7)), hTr + [t_wG], [ptok])
                    sk, stok = sgst_ring.next()
                    P.op("act", lambda e: e.activation(out=sgst[:, sk, :], in_=pa[pk][:], func=AF.Sigmoid, bias=b_g[:, gi:gi + 1]), [ptok, tb], [stok])
                    P.dma("pool" if gi % 2 else "sp", lambda e: e.dma_start(out=SG[mixer, oc, :, cols], in_=sgst[:, sk, :]), [stok], [], [t_SG])

                def s0():
                    for t in range(4):
                        cq_mm(t)
                    for m in range(4):
                        qfox(m)
                    ownF()

                def s1():
                    cq_T()
                    for gi in range(0, 5):
                        gate(gi)

                def s2():
                    for t in range(4):
                        qfull(t)
                    for gi in range(5, 11):
                        gate(gi)

                def s3():
                    q_T()
                    for gi in range(11, 16):
                        gate(gi)
                return [s0, s1, s2, s3]

            NCH = 20
            for t in range(4):
                sA_normA(0, t)
                sA_normB(0, t)
            sA_load(1)
            rope_hold["cs"], rope_hold["tok"] = rope_table(sb1, "rb", posb_h, 64)
            for ci in range(NCH):
                slots = p1_slots(ci) if ci < 16 else p2_slots(ci - 16)
                for t in range(4):
                    if ci + 1 < NCH:
                        sA_normA(ci + 1, t)
                    drain_split(1)
                    slots[t]()
                    if ci + 1 < NCH:
                        sA_normB(ci + 1, t)
                if ci + 2 < NCH:
                    sA_load(ci + 2)
                if 1 <= ci <= 9:
                    emit_p2_prefetch(3)
                if ci == 15 or ci == NCH - 1:
                    drain_split(8)
                if ci == 15:
                    P.barrier()
                    esA1.close()
                    p2_setup()
            P.barrier()
            p2["es"].close()

        with contextlib.ExitStack() as esC2:
            sbC2 = mk_alloc(esC2)
            wff1 = sbC2("wff1", [128, 8, DFF], BF16)
            t_wff1 = Tok()
            with contextlib.ExitStack() as esC1:
                sbC1 = mk_alloc(esC1)
                wom = sbC1("wom", [128, 4, D], BF16)
                wof = sbC1("wof", [128, 4, D], BF16)
                wout = sbC1("wout", [128, 8, D], BF16)
                t_wom, t_wof, t_wout = Tok(), Tok(), Tok()

                w3_pref = []
                for kc in range(4):
                    w3_pref.append((lambda e, kc=kc: e.dma_start(out=wom[:, kc, :], in_=w_om_h.ap()[kc * 128:(kc + 1) * 128, :]), t_wom))
                    w3_pref.append((lambda e, kc=kc: e.dma_start(out=wof[:, kc, :], in_=w_of_h.ap()[kc * 128:(kc + 1) * 128, :]), t_wof))
                for kc in range(8):
                    w3_pref.append((lambda e, kc=kc: e.dma_start(out=wout[:, kc, :], in_=w_out_h.ap()[kc * 128:(kc + 1) * 128, :]), t_wout))
                for kc in range(8):
                    for hf in range(2):
                        w3_pref.append((lambda e, kc=kc, hf=hf: e.dma_start(out=wff1[:, kc, hf * 2048:(hf + 1) * 2048],
                                                                            in_=w_ff1_h.ap()[kc * 128:(kc + 1) * 128, hf * 2048:(hf + 1) * 2048]), t_wff1))
                for kc in range(32):
                    w3_pref.append((lambda e, kc=kc: e.dma_start(out=FF2b[kc * 128:(kc + 1) * 128, :], in_=w_ff2_h.ap()[kc * 128:(kc + 1) * 128, :]), t_FF2b))

                def emit_w3_loads(k):
                    for _ in range(k):
                        if w3_pref:
                            fn_, tk_ = w3_pref.pop(0)
                            P.dma("pool", fn_, [], [], [tk_])

                with contextlib.ExitStack() as esB:
                    sbB = mk_alloc(esB)
                    SP = [esB.enter_context(nc.psum_tensor(f"sp{k}", [128, 1024], F32)) for k in range(3)]
                    OA = [esB.enter_context(nc.psum_tensor(f"oa{k}", [128, 512], F32)) for k in range(2)]
                    KT = sbB("KT", [96, 2, S], BF16)
                    VV = sbB("VV", [128, 2, 64 * 65], BF16)
                    QQ = sbB("QQ", [96, 2, NOWN], BF16)
                    kvq_toks = [[Tok(), Tok(), Tok()] for _ in range(2)]
                    maskT = sbB("maskTs", [128, 16, 128], BF16)
                    t_mask = Tok()
                    P.dma("sp", lambda e: e.dma_start(out=maskT[:].rearrange("p a b -> p (a b)"), in_=mask_h.ap()), [], [t_mask])
                    NPT = 4
                    PT = sbB("PT", [128, NPT, 1024], BF16)
                    PT_ring = Ring(NPT)
                    osb = sbB("osb", [65, 2, 512], F32)
                    osb_ring = Ring(2)
                    rcp = sbB("rcp", [65, 2, 512], F32)
                    rcp_ring = Ring(2)
                    ost = sbB("ost", [64, 2, 512], BF16)
                    ost_ring = Ring(2)
                    SP_ring = Ring(3)
                    O_ring = Ring(2)
                    units = [(0, h) for h in range(8)] + [(1, h) for h in range(8)]

                    def emit_loads(u):
                        mixer, h = units[u]
                        s_ = u % 2
                        dk = 96 if mixer == 0 else 70
                        Ksrc, Vsrc, Qsrc = (KTm, Vm, QTm) if mixer == 0 else (KTf, Vf, QTf)
                        tK, tV, tQ = kvq_toks[s_]
                        rK = [t_KTm] if mixer == 0 else [t_KTf]
                        rV = [t_Vm] if mixer == 0 else [t_Vf]
                        rQ = [t_QTm] if mixer == 0 else [t_QTf]
                        P.dma("sp", lambda e: e.dma_start(out=QQ[0:dk, s_, :], in_=Qsrc[h, 0:dk, :]), rQ, [tQ])
                        for q4 in range(4):
                            P.dma("sp", lambda e: e.dma_start(out=KT[0:dk, s_, q4 * 2048:(q4 + 1) * 2048], in_=Ksrc[h, 0:dk, q4 * 2048:(q4 + 1) * 2048]), rK,
                                  ([tK] if q4 == 0 else []), ([] if q4 == 0 else [tK]))
                        for v2 in range(2):
                            P.dma("sp", lambda e: e.dma_start(out=VV[:, s_, v2 * 2080:(v2 + 1) * 2080], in_=Vsrc[h, :, v2 * 2080:(v2 + 1) * 2080]), rV,
                                  ([tV] if v2 == 0 else []), ([] if v2 == 0 else [tV]))

                    deferred = []

                    def finalize(u, j, ok, otok):
                        mixer, h = units[u]
                        sk, stok = osb_ring.next()
                        P.op("dve", lambda e: e.tensor_copy(out=osb[:, sk, :], in_=OA[ok][0:65, :]), [otok], [stok])
                        rk, rtok = rcp_ring.next()
                        P.op("dve", lambda e: e.reciprocal(out=rcp[64:65, rk, :], in_=osb[64:65, sk, :]), [stok], [rtok])

                        def pe_part():
                            P.op("pe", lambda e: e.matmul(OA[ok][0:64, :], lhsT=onesf[64:65, 0:64], rhs=rcp[64:65, rk, :], start=True, stop=True), [rtok, t_onesf], [otok])
                            tk_, ttok = ost_ring.next()
                            P.op("dve", lambda e: e.tensor_tensor(out=ost[:, tk_, :], in0=osb[0:64, sk, :], in1=OA[ok][0:64, :], op=ALU.mult), [stok, otok], [ttok])
                            P.dma("sp", lambda e: e.dma_start(out=OT[mixer, h * 64:(h + 1) * 64, j * 512:(j + 1) * 512], in_=ost[:, tk_, :]), [ttok], [], [t_OT])
                        deferred.append(pe_part)

                    def slot_groups(j):
                        groups = []
                        for p in range(8 * j):
                            groups.append([(2 * p, 0, 512, None, 0), (2 * p + 1, 0, 512, None, 512)])
                        base = 16 * j
                        for w0 in (0, 2):
                            groups.append([(base + w0, 0, 512, w0, 0), (base + w0 + 1, 0, 512, w0 + 1, 512)])
                        for w0 in (4, 6):
                            groups.append([(base + w0, 128, 384, w0, 0), (base + w0 + 1, 128, 384, w0 + 1, 512)])
                        groups.append([(base + 8 + q, 256, 256, 8 + q, 256 * q) for q in range(4)])
                        groups.append([(base + 12 + q, 384, 128, 12 + q, 128 * q) for q in range(4)])
                        return groups

                    LAG = 2
                    emit_loads(0)
                    glist = []
                    for u in range(16):
                        for j in range(4):
                            groups = slot_groups(j)
                            for gi, grp in enumerate(groups):
                                glist.append((u, j, grp, gi == 0, gi == len(groups) - 1))
                    infl = []
                    slot_state = {}
                    pend = []
                    for idx in range(len(glist) + LAG):
                        if idx == 6:
                            emit_loads(1)
                        if idx < len(glist):
                            u, j, grp, first, last = glist[idx]
                            mixer, h = units[u]
                            s_ = u % 2
                            dk = 96 if mixer == 0 else 70
                            scale = float(96 ** -0.5) if mixer == 0 else 0.125
                            tK, tV, tQ = kvq_toks[s_]
                            sp, sptok = SP_ring.next()
                            for bi, (kb, c0, N, w, off) in enumerate(grp):
                                win = w is not None
                                P.op("pe", lambda e: e.matmul(SP[sp][:, off:off + N], lhsT=KT[0:dk, s_, kb * 128:(kb + 1) * 128],
                                                              rhs=QQ[0:dk, s_, j * 512 + c0:(j + 1) * 512], start=True, stop=(not win)),
                                     [tK, tQ], ([sptok] if bi == 0 else []), ([] if bi == 0 else [sptok]))
                                if win:
                                    P.op("pe", lambda e: e.matmul(SP[sp][:, off:off + 128], lhsT=ident[:], rhs=maskT[:, w, :],
                                                                  start=False, stop=True), [t_ident, t_mask], [], [sptok])
                            pt_, pttok = PT_ring.next()
                            N0 = grp[0][2]
                            if len(grp) == 2 and N0 < 512:
                                src = SP[sp][:].rearrange("p (a b) -> p a b", a=2)[:, :, 0:N0]
                                dst = PT[:, pt_, :].rearrange("p (a b) -> p a b", a=2)[:, :, 0:N0]
                            else:
                                tot = N0 * len(grp)
                                src = SP[sp][:, 0:tot]
                                dst = PT[:, pt_, 0:tot]
                            P.op("act", lambda e: e.activation(out=dst, in_=src, func=AF.Exp, scale=scale), [sptok], [pttok])
                            infl.append((u, j, grp, pt_, pttok, first, last))
                        if idx >= LAG:
                            u, j, grp, pt_, pttok, first, last = infl[idx - LAG]
                            s_ = u % 2
                            tK, tV, tQ = kvq_toks[s_]
                            if first:
                                slot_state[(u, j)] = O_ring.next()
                            ok, otok = slot_state[(u, j)]
                            last_kb = 16 * j + 15
                            for (kb, c0, N, w, off) in grp:
                                P.op("pe", lambda e: e.matmul(OA[ok][0:65, c0:512], lhsT=VV[:, s_, kb * 65:(kb + 1) * 65], rhs=PT[:, pt_, off:off + N],
                                                              start=(kb == 0), stop=(kb == last_kb)), [pttok, tV], [otok])
                            if last:
                                finalize(u, j, ok, otok)
                                pend.append((idx + 5, deferred.pop()))
                                if j == 3 and u + 2 < 16:
                                    emit_loads(u + 2)
                                if u >= 1:
                                    emit_w3_loads(2)
                        while pend and pend[0][0] <= idx:
                            pend.pop(0)[1]()
                    while pend:
                        pend.pop(0)[1]()
                    emit_w3_loads(1000)
                    P.barrier()

                with contextlib.ExitStack() as es3a:
                    sb3 = mk_alloc(es3a)
                    pa = [es3a.enter_context(nc.psum_tensor(f"pc{k}", [128, 512], F32)) for k in range(6)]
                    ptb = [es3a.enter_context(nc.psum_tensor(f"pu{k}", [128, 1024], BF16)) for k in range(2)]
                    pa_ring = Ring(6)
                    pt_ring = Ring(2)
                    otm = sb3("otm", [128, 2, 2, 4, 512], BF16)
                    otm_toks = [Tok(), Tok()]
                    sgt = sb3("sgt", [128, 4, 2, 512], BF16)
                    sgt_ring = Ring(4)
                    mtmp = sb3("mtmp", [128, 2, 2, 512], F32)
                    mtmp_ring = Ring(2)
                    mgT = sb3("mgT", [128, 2, 8, 512], BF16)
                    mgT_toks = [[Tok() for _ in range(8)] for _ in range(2)]
                    xo_t = sb3("xo_t", [128, 2, D], F32)
                    xo_ring = Ring(2)
                    x1t = sb3("x1t", [128, 2, D], F32)
                    x1_ring = Ring(2)
                    h2n = sb3("h2n", [128, 3, D], BF16)
                    h2n_ring = Ring(3)
                    h2st = sb3("h2st", [128, 2, 8, 512], BF16)
                    h2st_toks = [[Tok() for _ in range(4)] for _ in range(2)]
                    sg_loaded = {}
                    gbcB, t_gbcB = load_gains(sb3, "gbcB", [1, 2])

                    def load_ot(j):
                        cols = slice(j * 512, (j + 1) * 512)
                        for mixer in range(2):
                            P.dma("sp", lambda e: e.dma_start(out=otm[:, j % 2, mixer], in_=OT[mixer, :, cols].rearrange("(kc p) t -> p kc t", p=128)),
                                  [t_OT], ([otm_toks[j % 2]] if mixer == 0 else []), ([] if mixer == 0 else [otm_toks[j % 2]]))

                    def load_sg(j, oc):
                        if j >= 4:
                            return
                        cols = slice(j * 512, (j + 1) * 512)
                        gk, gtok = sgt_ring.next()
                        for mixer in range(2):
                            P.dma("sp", lambda e: e.dma_start(out=sgt[:, gk, mixer, :], in_=SG[mixer, oc, :, cols]), [t_SG],
                                  ([gtok] if mixer == 0 else []), ([] if mixer == 0 else [gtok]))
                        sg_loaded[(j, oc)] = (gk, gtok)

                    def Y(j, oc):
                        if j >= 4:
                            return
                        os_ = j % 2
                        nxt = j * 8 + oc + 2
                        load_sg(nxt // 8, nxt % 8)
                        gk, gtok = sg_loaded[(j, oc)]
                        pk, ptok = pa_ring.next()
                        pk2, ptok2 = pa_ring.next()
                        for kc in range(4):
                            P.op("pe", lambda e: e.matmul(pa[pk][:], lhsT=wom[:, kc, oc * 128:(oc + 1) * 128], rhs=otm[:, os_, 0, kc, :],
                                                          start=(kc == 0), stop=(kc == 3)), [otm_toks[os_], t_wom], [ptok])
                        for kc in range(4):
                            P.op("pe", lambda e: e.matmul(pa[pk2][:], lhsT=wof[:, kc, oc * 128:(oc + 1) * 128], rhs=otm[:, os_, 1, kc, :],
                                                          start=(kc == 0), stop=(kc == 3)), [otm_toks[os_], t_wof], [ptok2])
                        mk_, mtok = mtmp_ring.next()
                        P.op("dve", lambda e: e.tensor_tensor(out=mtmp[:, mk_, 0, :], in0=pa[pk][:], in1=sgt[:, gk, 0, :], op=ALU.mult), [ptok, gtok], [mtok])
                        P.op("dve", lambda e: e.tensor_tensor(out=mtmp[:, mk_, 1, :], in0=pa[pk2][:], in1=sgt[:, gk, 1, :], op=ALU.mult), [ptok2, gtok], [], [mtok])
                        P.op("pool", lambda e: e.tensor_tensor(out=mgT[:, os_, oc, :], in0=mtmp[:, mk_, 0, :], in1=mtmp[:, mk_, 1, :], op=ALU.add),
                             [mtok], [mgT_toks[os_][oc]])

                    pendB = {}

                    def W(j, t):
                        os_ = j % 2
                        row0 = j * 512 + t * 128
                        xk, xtok = xo_ring.next()
                        P.dma("sp", lambda e: e.dma_start(out=xo_t[:, xk, :], in_=xo[row0:row0 + 128, :]), [], [xtok])
                        pks = []
                        for hf in range(2):
                            pk, ptok = pa_ring.next()
                            for kc in range(8):
                                P.op("pe", lambda e: e.matmul(pa[pk][:], lhsT=mgT[:, os_, kc, t * 128:(t + 1) * 128], rhs=wout[:, kc, hf * 512:(hf + 1) * 512],
                                                              start=(kc == 0), stop=(kc == 7)), [mgT_toks[os_][kc], t_wout], [ptok])
                            pks.append((pk, ptok))
                        st, st_tok = new_stat()
                        for hf in range(2):
                            pk, ptok = pks[hf]
                            P.op("act", lambda e: e.activation(out=junk[:, hf * 512:(hf + 1) * 512], in_=pa[pk][:], func=AF.Square, accum_out=st[:, hf:hf + 1]),
                                 [ptok], ([st_tok, t_junk[hf]] if hf == 0 else [t_junk[hf]]), ([] if hf == 0 else [st_tok]))
                        P.op("dve", lambda e: e.tensor_tensor(out=st[:, 0:1], in0=st[:, 0:1], in1=st[:, 1:2], op=ALU.add), [st_tok], [st_tok], n=1)
                        rstd_from_ss(st[:, 0:1], D, [st_tok], st[:, 2:4], st_tok)
                        x1k, x1tok = x1_ring.next()
                        for hf in range(2):
                            pk, ptok = pks[hf]
                            P.op("dve", lambda e: e.scalar_tensor_tensor(out=x1t[:, x1k, hf * 512:(hf + 1) * 512], in0=pa[pk][:], scalar=st[:, 2:3],
                                                                        in1=gbcB[:, 0, hf * 512:(hf + 1) * 512], op0=ALU.mult, op1=ALU.mult),
                                 [ptok, st_tok, t_gbcB], ([x1tok] if hf == 0 else []), ([] if hf == 0 else [x1tok]))
                        P.op("pool", lambda e: e.tensor_tensor(out=x1t[:, x1k, :], in0=x1t[:, x1k, :], in1=xo_t[:, xk, :], op=ALU.add), [x1tok, xtok], [x1tok])
                        P.dma("sp", lambda e: e.dma_start(out=X1[row0:row0 + 128, :], in_=x1t[:, x1k, :]), [x1tok], [], [t_X1])
                        pendB[(j, t)] = normA(x1t[:, x1k, :], x1tok, gbcB[:, 1, :], t_gbcB, h2n, h2n_ring)

                    def WB(j, t):
                        os_ = j % 2
                        hk, htok = pendB[(j, t)]
                        normB(h2n, hk, htok, h2st[:, os_, :, t * 128:(t + 1) * 128], h2st_toks[os_][t], ptb, pt_ring)
                        if t == 3:
                            P.dma("sp", lambda e: e.dma_start(out=H2T[j].rearrange("p (c t) -> p c t", c=8), in_=h2st[:, os_]), h2st_toks[os_], [], [t_H2T])

                    load_ot(0)
                    load_sg(0, 0)
                    load_sg(0, 1)
                    for oc in range(8):
                        Y(0, oc)
                    for j in range(4):
                        if j + 1 < 4:
                            load_ot(j + 1)
                        for t in range(4):
                            W(j, t)
                            Y(j + 1, 2 * t)
                            Y(j + 1, 2 * t + 1)
                            WB(j, t)
                    P.barrier()

            with contextlib.ExitStack() as es3b:
                sb4 = mk_alloc(es3b)
                pa = [es3b.enter_context(nc.psum_tensor(f"pd{k}", [128, 512], F32)) for k in range(6)]
                pa_ring = Ring(6)
                wff2 = sb4("wff2", [128, 32, D], BF16)
                t_wff2 = Tok()
                h2T = sb4("h2T", [128, 8, 512], BF16)
                t_h2T = Tok()
                aT = sb4("aT", [128, 32, 512], BF16)
                aT_toks = [Tok() for _ in range(32)]
                rl = sb4("rl", [128, 2, 512], BF16)
                rl_ring = Ring(2)
                x1l = sb4("x1l", [128, 2, D], F32)
                x1l_ring = Ring(2)
                yo = sb4("yo", [128, 2, D], F32)
                yo_ring = Ring(2)

                gbcC, t_gbcC = load_gains(sb4, "gbcC", [3])

                def load_h2(j):
                    P.dma("sp", lambda e: e.dma_start(out=h2T[:], in_=H2T[j].rearrange("p (c t) -> p c t", c=8)), [t_H2T], [t_h2T])

                def load_x1(row0):
                    xk, xtok = x1l_ring.next()
                    P.dma("sp", lambda e: e.dma_start(out=x1l[:, xk, :], in_=X1[row0:row0 + 128, :]), [t_X1], [xtok])
                    return xk, xtok

                load_h2(0)
                for kc in range(32):
                    P.dma("sp", lambda e: e.dma_start(out=wff2[:, kc, :], in_=FF2b[kc * 128:(kc + 1) * 128, :]), [t_FF2b], [], [t_wff2])
                for j in range(4):
                    for oc in range(32):
                        pk, ptok = pa_ring.next()
                        for fc in range(8):
                            P.op("pe", lambda e: e.matmul(pa[pk][:], lhsT=wff1[:, fc, oc * 128:(oc + 1) * 128], rhs=h2T[:, fc, :],
                                                          start=(fc == 0), stop=(fc == 7)), [t_h2T, t_wff1], [ptok])
                        rk, rtok = rl_ring.next()
                        P.op("act", lambda e: e.activation(out=rl[:, rk, :], in_=pa[pk][:], func=AF.Relu), [ptok], [rtok])
                        P.op("pool" if oc % 2 else "dve", lambda e: e.tensor_tensor(out=aT[:, oc, :], in0=rl[:, rk, :], in1=rl[:, rk, :], op=ALU.mult), [rtok], [aT_toks[oc]])
                    if j + 1 < 4:
                        load_h2(j + 1)
                    nxt_x = load_x1(j * 512)
                    for t in range(4):
                        row0 = j * 512 + t * 128
                        xk, xtok = nxt_x
                        if t + 1 < 4:
                            nxt_x = load_x1(row0 + 128)
                        pks = []
                        for hf in range(2):
                            pk, ptok = pa_ring.next()
                            for kc in range(32):
                                P.op("pe", lambda e: e.matmul(pa[pk][:], lhsT=aT[:, kc, t * 128:(t + 1) * 128], rhs=wff2[:, kc, hf * 512:(hf + 1) * 512],
                                                              start=(kc == 0), stop=(kc == 31)), [aT_toks[kc], t_wff2], [ptok])
                            pks.append((pk, ptok))
                        st, st_tok = new_stat()
                        for hf in range(2):
                            pk, ptok = pks[hf]
                            P.op("act", lambda e: e.activation(out=junk[:, hf * 512:(hf + 1) * 512], in_=pa[pk][:], func=AF.Square, accum_out=st[:, hf:hf + 1]),
                                 [ptok], ([st_tok, t_junk[hf]] if hf == 0 else [t_junk[hf]]), ([] if hf == 0 else [st_tok]))
                        P.op("dve", lambda e: e.tensor_tensor(out=st[:, 0:1], in0=st[:, 0:1], in1=st[:, 1:2], op=ALU.add), [st_tok], [st_tok], n=1)
                        rstd_from_ss(st[:, 0:1], D, [st_tok], st[:, 2:4], st_tok)
                        yk, ytok = yo_ring.next()
                        for hf in range(2):
                            pk, ptok = pks[hf]
                            P.op("dve", lambda e: e.scalar_tensor_tensor(out=yo[:, yk, hf * 512:(hf + 1) * 512], in0=pa[pk][:], scalar=st[:, 2:3],
                                                                        in1=gbcC[:, 0, hf * 512:(hf + 1) * 512], op0=ALU.mult, op1=ALU.mult),
                                 [ptok, st_tok, t_gbcC], ([ytok] if hf == 0 else []), ([] if hf == 0 else [ytok]))
                        P.op("pool", lambda e: e.tensor_tensor(out=yo[:, yk, :], in0=yo[:, yk, :], in1=x1l[:, xk, :], op=ALU.add), [ytok, xtok], [ytok])
                        P.dma("sp", lambda e: e.dma_start(out=outp[row0:row0 + 128, :], in_=yo[:, yk, :]), [ytok], [], [t_out])
                P.op("sp", lambda e: None, [t_out], [], track=False)

                P.reschedule(RESCHED_SEGS)
                run_engine = P.emit(sems)
                with nc.Block() as block:
                    @block.sync
                    def _(e):
                        run_engine("sp", e)

                    @block.tensor
                    def _(e):
                        run_engine("pe", e)

                    @block.scalar
                    def _(e):
                        run_engine("act", e)

                    @block.vector
                    def _(e):
                        run_engine("dve", e)

                    @block.gpsimd
                    def _(e):
                        run_engine("pool", e)
    return nc


def _host_consts(g):
    w = np.arange(16)
    kpos = w[:, None] * 128 + np.arange(128)[None, :]
    qpos = ((4 * (w // 4) + g) * 128)[:, None] + np.arange(128)[None, :]
    m = np.where(kpos[:, :, None] <= qpos[:, None, :], 0.0, MASKV).astype(np.float32)
    maskT = np.ascontiguousarray(m.transpose(1, 0, 2).reshape(128, 16 * 128)).astype(ml_dtypes.bfloat16)
    invf = (10000.0 ** (-np.arange(16, dtype=np.float32) / 16.0)).astype(np.float32)
    rc = np.concatenate([invf, invf, np.zeros(16, np.float32), np.full(16, np.pi / 2, np.float32)])
    ropec = np.ascontiguousarray(np.broadcast_to(rc[None, :], (128, 64))).astype(np.float32)
    sel = np.zeros((8, 4), np.float32)
    sel[:, g] = 1.0
    return maskT, ropec, sel


_NC_CACHE = {}


def kernel(x, positions, ln_pre_mix, ln_post_mix, ln_pre_mlp, ln_post_mlp, w_in, b_in, q_a_norm, w_uq,
           kv_a_norm, w_uk, w_uv, w_o_mla, w_o_fox, w_out, w_ff1, w_ff2, _dbg=False):
    x = np.asarray(x, np.float32)
    positions = np.asarray(positions, np.int32)
    if _dbg not in _NC_CACHE:
        _NC_CACHE[_dbg] = build_program(dbg=_dbg)
    nc = _NC_CACHE[_dbg]
    f32 = lambda a: np.ascontiguousarray(np.asarray(a, np.float32))
    shared = {
        "w_in": f32(w_in[0]), "b_in": f32(b_in[0]),
        "ln_pre_mix": f32(ln_pre_mix[0]), "ln_post_mix": f32(ln_post_mix[0]),
        "ln_pre_mlp": f32(ln_pre_mlp[0]), "ln_post_mlp": f32(ln_post_mlp[0]),
        "q_a_norm": f32(q_a_norm[0]), "kv_a_norm": f32(kv_a_norm[0]),
        "w_uq": f32(w_uq[0]), "w_uk": f32(w_uk[0]), "w_uv": f32(w_uv[0]),
        "w_o_mla": f32(w_o_mla[0]), "w_o_fox": f32(w_o_fox[0]), "w_out": f32(w_out[0]),
        "w_ff1": f32(w_ff1[0]), "w_ff2": f32(w_ff2[0]),
    }
    in_maps = []
    own_idx = []
    for c in range(8):
        b, g = c // 4, c % 4
        idx = np.concatenate([np.arange((4 * m_ + g) * 128, (4 * m_ + g + 1) * 128) for m_ in range(16)])
        own_idx.append((b, idx))
        maskT, ropec, sel = _host_consts(g)
        m = dict(shared)
        m["xb"] = np.ascontiguousarray(x[b])
        m["xo"] = np.ascontiguousarray(x[b][idx])
        m["posb"] = np.ascontiguousarray(positions[b].reshape(64, 128).T)
        m["poso"] = np.ascontiguousarray(positions[b][idx].reshape(16, 128).T)
        m["sel"] = sel
        m["maskT"] = maskT
        m["ropec"] = ropec
        in_maps.append(m)
    res = run_bass_kernel_spmd(nc, in_maps, core_ids=list(range(8)))
    out = np.empty((2, S, D), np.float32)
    for c in range(8):
        b, idx = own_idx[c]
        out[b, idx] = np.asarray(res.results[c]["out"], np.float32)
    if _dbg:
        return out, res.results
    return out
```
